# Optimizing a Trainium2 kernel written in Bass

```python
import jax, jax.numpy as jnp
from jax import lax
import numpy as np

D_MODEL = 1024
BATCH = 2
SEQ = 8192
DEPTH = 1

D_MIX = D_MODEL
D_A = D_MIX // 2
D_B = D_MIX - D_A
N_HEADS_A = 8
HEAD_DIM_A = D_A // N_HEADS_A
N_GROUPS_B = 8
GROUP_DIM_B = D_B // N_GROUPS_B
CHUNK = 128
CONV_WIDTH = 31
D_IN = 2 * D_A + 2 * D_B
D_FF = -(-8 * D_MODEL // (3 * 256)) * 256
N_MOD = 6
EPS = 1e-6

kernel_name = "hybrid_sgu_conformer_conv_adaln_block"


def rmsnorm(x, g):
    xf = x.astype(jnp.float32)
    y = xf * lax.rsqrt(jnp.mean(xf * xf, axis=-1, keepdims=True) + EPS)
    return (y * g.astype(jnp.float32)).astype(x.dtype)


def layernorm(x, g, b):
    xf = x.astype(jnp.float32)
    mu = jnp.mean(xf, axis=-1, keepdims=True)
    xc = xf - mu
    var = jnp.mean(xc * xc, axis=-1, keepdims=True)
    y = xc * lax.rsqrt(var + EPS) * g.astype(jnp.float32) + b.astype(jnp.float32)
    return y.astype(x.dtype)


def modulate(h, shift, scale):
    return h * (1 + scale[:, None, :]) + shift[:, None, :]


def spatial_gating_mixer(u, v, ln_g, ln_b, w_s, b_s):
    B, S, _ = v.shape
    v = layernorm(v, ln_g, ln_b)
    v = v.reshape(B, S // CHUNK, CHUNK, N_HEADS_A, HEAD_DIM_A)
    causal = jnp.tril(jnp.ones((CHUNK, CHUNK), dtype=bool))
    w = jnp.where(causal[None], w_s, jnp.zeros_like(w_s)).astype(v.dtype)
    mixed = jnp.einsum('hts,bcshd->bcthd', w, v) + b_s.T.astype(v.dtype)[None, None, :, :, None]
    return u * mixed.reshape(B, S, D_A)


def conformer_conv_mixer(val, gate, conv_w, conv_b, gn_g, gn_b):
    B, S, _ = val.shape
    y = val * jax.nn.sigmoid(gate)
    y = lax.conv_general_dilated(
        y, conv_w[:, None, :].astype(y.dtype), window_strides=(1,),
        padding=[(CONV_WIDTH - 1, 0)],
        dimension_numbers=('NWC', 'WIO', 'NWC'),
        feature_group_count=D_B) + conv_b.astype(y.dtype)
    y = y.reshape(B, S, N_GROUPS_B, GROUP_DIM_B)
    y = layernorm(y, gn_g.reshape(N_GROUPS_B, GROUP_DIM_B), gn_b.reshape(N_GROUPS_B, GROUP_DIM_B))
    return jax.nn.silu(y.reshape(B, S, D_B))


def setup_inputs(seed: int = 0) -> dict:
    key = jax.random.key(seed)
    ks = jax.random.split(key, 24)
    f32 = jnp.float32
    nrm = lambda k, shape, s: jax.random.normal(k, shape, f32) * s
    L = DEPTH
    return {
        "x": jax.random.normal(ks[0], (BATCH, SEQ, D_MODEL), f32),
        "c": jax.random.normal(ks[1], (BATCH, D_MODEL), f32),
        "ada_w": nrm(ks[2], (L, D_MODEL, N_MOD * D_MODEL), D_MODEL ** -0.5),
        "ada_b": nrm(ks[3], (L, N_MOD * D_MODEL), 0.02),
        "norm1_g": 1.0 + nrm(ks[4], (L, D_MODEL), 0.05),
        "w_in": nrm(ks[5], (L, D_MODEL, D_IN), D_MODEL ** -0.5),
        "b_in": nrm(ks[6], (L, D_IN), 0.02),
        "a_ln_g": 1.0 + nrm(ks[7], (L, D_A), 0.05),
        "a_ln_b": nrm(ks[8], (L, D_A), 0.02),
        "a_spatial_w": nrm(ks[9], (L, N_HEADS_A, CHUNK, CHUNK), CHUNK ** -0.5),
        "a_spatial_b": 1.0 + nrm(ks[10], (L, N_HEADS_A, CHUNK), 0.1),
        "b_conv_w": nrm(ks[11], (L, CONV_WIDTH, D_B), CONV_WIDTH ** -0.5),
        "b_conv_b": nrm(ks[12], (L, D_B), 0.02),
        "b_gn_g": 1.0 + nrm(ks[13], (L, D_B), 0.05),
        "b_gn_b": nrm(ks[14], (L, D_B), 0.02),
        "out_norm_a_g": 1.0 + nrm(ks[15], (L, D_A), 0.05),
        "out_norm_b_g": 1.0 + nrm(ks[16], (L, D_B), 0.05),
        "w_out": nrm(ks[17], (L, D_MIX, D_MODEL), D_MIX ** -0.5),
        "norm2_g": 1.0 + nrm(ks[18], (L, D_MODEL), 0.05),
        "w_ffn_in": nrm(ks[19], (L, D_MODEL, 2 * D_FF), D_MODEL ** -0.5),
        "w_ffn_out": nrm(ks[20], (L, D_FF, D_MODEL), D_FF ** -0.5),
        "ada_f_w": nrm(ks[21], (D_MODEL, 2 * D_MODEL), D_MODEL ** -0.5),
        "ada_f_b": nrm(ks[22], (2 * D_MODEL,), 0.02),
        "norm_f_g": 1.0 + nrm(ks[23], (D_MODEL,), 0.05),
    }


def reference(x, c, ada_w, ada_b, norm1_g, w_in, b_in, a_ln_g, a_ln_b, a_spatial_w,
              a_spatial_b, b_conv_w, b_conv_b, b_gn_g, b_gn_b, out_norm_a_g, out_norm_b_g,
              w_out, norm2_g, w_ffn_in, w_ffn_out, ada_f_w, ada_f_b, norm_f_g):
    c_act = jax.nn.silu(c)
    for i in range(DEPTH):
        cond = c_act @ ada_w[i] + ada_b[i]
        shift1, scale1, gate1, shift2, scale2, gate2 = jnp.split(cond, N_MOD, axis=-1)

        h = modulate(rmsnorm(x, norm1_g[i]), shift1, scale1)
        z = h @ w_in[i] + b_in[i]
        u, v, val, gate = jnp.split(z, [D_A, 2 * D_A, 2 * D_A + D_B], axis=-1)
        y_a = spatial_gating_mixer(jax.nn.gelu(u, approximate=False),
                                   jax.nn.gelu(v, approximate=False),
                                   a_ln_g[i], a_ln_b[i], a_spatial_w[i], a_spatial_b[i])
        y_b = conformer_conv_mixer(val, gate, b_conv_w[i], b_conv_b[i], b_gn_g[i], b_gn_b[i])
        y = jnp.concatenate([rmsnorm(y_a, out_norm_a_g[i]), rmsnorm(y_b, out_norm_b_g[i])], axis=-1)
        x = x + gate1[:, None, :] * (y @ w_out[i])

        h2 = modulate(rmsnorm(x, norm2_g[i]), shift2, scale2)
        g_ff, up_ff = jnp.split(h2 @ w_ffn_in[i], 2, axis=-1)
        x = x + gate2[:, None, :] * ((jax.nn.silu(g_ff) * up_ff) @ w_ffn_out[i])

    shift_f, scale_f = jnp.split(c_act @ ada_f_w + ada_f_b, 2, axis=-1)
    return modulate(rmsnorm(x, norm_f_g), shift_f, scale_f)
```

```python
import numpy as np
import concourse.bass as bass
import concourse.mybir as mybir
from concourse.bass_utils import run_bass_kernel_spmd

F32 = mybir.dt.float32
BF16 = mybir.dt.bfloat16
AF = mybir.ActivationFunctionType
ALU = mybir.AluOpType
AX = mybir.AxisListType

D = 1024
TOK = 2048
ST = 512
NST = TOK // ST
HALO = 32
DFF = 2816
NJP = 11
EPS = 1e-6
NCORES = 8

C_G1, C_G2, C_BB, C_CB, C_GNG, C_GNB, C_GA, C_GB, C_CC, C_HM = 0, 8, 16, 24, 28, 32, 36, 40, 44, 52
C_AB = 53
C_CW = 85
NCOL = C_CW + 124
R_LNG, R_LNB, R_BS, R_GF, R_ABG1, R_ABG2, R_ABSF, R_ABSCF, R_BINA = 0, 512, 1024, 1536, 2560, 3584, 4608, 5632, 6656
NROW = 7680


import os
KSTOP = int(os.environ.get("KSTOP", "99"))


class StopBuild(Exception):
    pass


def stop_at(n):
    if KSTOP == n:
        raise StopBuild()


class Buf:
    __slots__ = ("w", "r", "name")

    def __init__(self, name=""):
        self.w = None
        self.r = []
        self.name = name


class Eng:
    def __init__(self, name, h, sem, is_pe=False):
        self.name, self.h, self.sem, self.is_pe = name, h, sem, is_pe
        self.count = 0
        self.seen = {}


class DSem:
    def __init__(self, h):
        self.h = h
        self.count = 0


class Prog:
    def __init__(self, nc):
        self.nc = nc
        self._sems = []
        mk = lambda n: self._sem(n)
        self.pe = Eng("pe", nc.tensor, mk("s_pe"), True)
        self.act = Eng("act", nc.scalar, mk("s_act"))
        self.dve = Eng("dve", nc.vector, mk("s_dve"))
        self.pool = Eng("pool", nc.gpsimd, mk("s_pool"))
        self.sp = Eng("sp", nc.sync, mk("s_sp"))
        self.engs = [self.pe, self.act, self.dve, self.pool, self.sp]
        self.dsems = []

    def _sem(self, name):
        cm = self.nc.semaphore(name)
        h = cm.__enter__()
        self._sems.append(cm)
        return h

    def dsem(self, name):
        d = DSem(self._sem(name))
        self.dsems.append(d)
        return d

    def _wait(self, eng, deps):
        need = {}
        for (sem, val) in deps:
            if eng.is_pe and sem is eng.sem:
                continue
            if eng.seen.get(id(sem), 0) >= val:
                continue
            if need.get(id(sem), (None, 0))[1] < val:
                need[id(sem)] = (sem, val)
        for k, (sem, val) in need.items():
            eng.h.wait_ge(sem, val)
            eng.seen[k] = val

    def _deps(self, reads, writes):
        deps = []
        for b in reads:
            if b.w is not None:
                deps.append(b.w)
        for b in writes:
            if b.w is not None:
                deps.append(b.w)
            deps.extend(b.r)
        return deps

    def emit(self, eng, fn, reads=(), writes=(), inc=True):
        self._wait(eng, self._deps(reads, writes))
        ins = fn(eng.h)
        if inc:
            eng.count += 1
            ins.then_inc(eng.sem, 1)
            tok = (eng.sem, eng.count)
        else:
            tok = (eng.sem, eng.count + 1)
        for b in reads:
            b.r.append(tok)
        for b in writes:
            b.w = tok
            b.r = []
        return tok

    def dma(self, q, out_ap, in_ap, dsem, reads=(), writes=()):
        self._wait(q, self._deps(reads, writes))
        ins = q.h.dma_start(out=out_ap, in_=in_ap)
        dsem.count += 16
        ins.then_inc(dsem.h, 16)
        tok = (dsem.h, dsem.count)
        for b in reads:
            b.r.append(tok)
        for b in writes:
            b.w = tok
            b.r = []
        return tok

    def barrier(self):
        toks = [(e.sem, e.count) for e in self.engs if e.count > 0]
        toks += [(d.h, d.count) for d in self.dsems if d.count > 0]
        for e in self.engs:
            self._wait(e, toks)

    def close(self):
        for cm in reversed(self._sems):
            cm.__exit__(None, None, None)


def build_nc():
    nc = bass.Bass("TRN2", target_bir_lowering=False)
    dr = lambda n, shp, kind="ExternalInput": nc.dram_tensor(n, shp, F32, kind=kind).ap()
    x_d = dr("x", [TOK, D])
    xh_d = dr("xh", [HALO, D])
    cols_d = dr("cols", [128, NCOL])
    rows_d = dr("rows", [128, NROW])
    wsp_d = dr("wsp", [128, 8, 128])
    adaw_d = dr("ada_w", [D, 6 * D])
    adafw_d = dr("ada_f_w", [D, 2 * D])
    win_d = dr("w_in", [D, 2 * D])
    wout_d = dr("w_out", [D, D])
    wfi_d = dr("w_ffn_in", [D, 2 * DFF])
    wfo_d = dr("w_ffn_out", [DFF, D])
    y_d = dr("y", [TOK, D], kind="ExternalOutput")
    scr_d = dr("cond_scr", [128, 3 * D], kind="Internal")
    scr_wi = nc.dram_tensor("scr_wi", [6, 128, 8, 2, 512], BF16, kind="Internal").ap()
    scr_wo = nc.dram_tensor("scr_wo", [128, 22, D], BF16, kind="Internal").ap()

    P = Prog(nc)
    pe, act, dve, pool, sp = P.pe, P.act, P.dve, P.pool, P.sp
    ctxs = []

    def sb(name, shape, dt=F32):
        cm = nc.sbuf_tensor("s_" + name, shape, dt)
        t = cm.__enter__()
        ctxs.append(cm)
        return t

    banks = []
    for i in range(8):
        cm = nc.psum_tensor(f"bank{i}", [128, 512], F32)
        banks.append(cm.__enter__())
        ctxs.append(cm)
    bank_bufs = [Buf(f"bank{i}") for i in range(8)]
    rr = [0]

    rr_n = [5]

    def palloc():
        i = rr[0] % rr_n[0]
        rr[0] += 1
        return banks[i], bank_bufs[i]

    cols = sb("cols", [128, NCOL]); b_cols = Buf("cols")
    identf = sb("identf", [128, 128]); b_identf = Buf()
    identb = sb("identb", [128, 128], BF16); b_identb = Buf()
    jb = sb("jb", [128, 128], BF16); b_jb = Buf()
    onesm = sb("onesm", [128, 128], BF16); b_onesm = Buf()
    ones1 = sb("ones1", [1, 128], BF16); b_ones1 = Buf()
    ccols = sb("ccols", [128, 36]); b_ccols = Buf()
    CS1, CA1, CS2, CA2, CCBP = 0, 8, 16, 24, 32
    x_sb = sb("x_sb", [128, 16, D]); b_x = [Buf(f"x{i}") for i in range(16)]
    stat = sb("stat", [128, 64]); b_stat = [Buf(f"stat{i}") for i in range(16)]

    d_cols = P.dsem("d_cols"); d_rowc = P.dsem("d_rowc"); d_stage = P.dsem("d_stage"); d_xh = P.dsem("d_xh")
    d_wspf = P.dsem("d_wspf"); d_wstage = P.dsem("d_wstage"); d_ostage = P.dsem("d_ostage")
    d_x = [P.dsem(f"d_x{i}") for i in range(NST)]
    d_w = P.dsem("d_w")
    d_ring = [P.dsem(f"d_ring{i}") for i in range(4)]
    d_out = [P.dsem(f"d_out{i}") for i in range(2)]
    d_cast = P.dsem("d_cast")

    try:
        P.dma(sp, cols[:], cols_d[:, :], d_cols, writes=[b_cols])

        P.emit(pool, lambda e: e.memset(identf[:], 0.0), writes=[b_identf])
        P.emit(pool, lambda e: e.affine_select(out=identf[:], in_=identf[:], pattern=[[-1, 128]],
                                               compare_op=ALU.not_equal, fill=1.0, base=0, channel_multiplier=1),
               reads=[b_identf], writes=[b_identf])
        P.emit(dve, lambda e: e.tensor_copy(out=identb[:], in_=identf[:]), reads=[b_identf], writes=[b_identb])
        P.emit(pool, lambda e: e.memset(jb[:], 0.0), writes=[b_jb])
        P.emit(pool, lambda e: e.memset(jb[0:64, 0:64], 1.0 / 64), writes=[b_jb])
        P.emit(pool, lambda e: e.memset(jb[64:128, 64:128], 1.0 / 64), writes=[b_jb])
        P.emit(pool, lambda e: e.memset(onesm[:], 1.0 / 512), writes=[b_onesm])
        P.emit(pool, lambda e: e.memset(ones1[:], 1.0), writes=[b_ones1])

        w_in_sb = sb("w_in_sb", [128, 8, 2 * D], BF16); b_win = Buf()
        w_out_sb = sb("w_out_sb", [128, 8, D], BF16); b_wout = [Buf() for _ in range(8)]
        convL = sb("convL", [128, 4, 31, 128], BF16); b_convL = Buf()
        wct = sb("wct", [128, 8, 128], BF16); b_wct = Buf()
        lng_bc = sb("lng_bc", [128, 512]); lnb_bc = sb("lnb_bc", [128, 512]); bs_full = sb("bs_full", [128, 512])
        b_rowc = Buf()
        binhi = sb("binhi", [1, D], BF16); b_bin = Buf()
        caH = sb("caH", [128, 8, 128], BF16); b_caH = Buf()
        tmpR = sb("tmpR", [128, 256]); b_tmpR = Buf()
        stage = sb("stage", [128, 512]); b_stage = Buf()
        xn = sb("xn", [128, 2, D], BF16); b_xn = [Buf(), Buf()]
        sq2 = sb("sq2", [128, 2, 512], BF16)
        dsq = sq2[:, 0, :]; ybsq = sq2[:, 1, :]; junk = sq2[:].rearrange("p a b -> p (a b)")
        junkx = sb("junkx", [128, D], BF16); b_junkx = Buf()
        b_dsq = Buf(); b_ybsq = Buf()
        actT = sb("actT", [128, 8, ST], BF16); b_actT = [Buf(f"actT{k}") for k in range(8)]
        gu4 = sb("gu4", [128, 4, 512], BF16); b_gu = [Buf() for _ in range(4)]
        gv4 = sb("gv4", [128, 4, 512], BF16); b_gv = [Buf() for _ in range(4)]
        vln = sb("vln", [128, 512], BF16); b_vln = Buf()
        ya = sb("ya", [128, 512]); b_ya = Buf()
        yan2 = sb("yan2", [128, 2, 512], BF16); b_yan2 = [Buf(), Buf()]
        ybuf = sb("ybuf", [128, 4, HALO + ST], BF16); b_ybuf = [Buf() for _ in range(4)]
        sig = sb("sig", [128, 512]); b_sig = Buf()
        vtmp = sig; b_vtmp = b_sig
        lnv = sb("lnv", [128, 512]); b_lnv = Buf()
        dsq2 = sb("dsq2", [128, 2, 512], BF16); b_dsq2 = [Buf(), Buf()]
        cnb = sb("cnb", [128, 512]); b_cnb = Buf()

        P.dma(sp, lng_bc[:], rows_d[:, R_LNG:R_LNG + 512], d_rowc, writes=[b_rowc])
        P.dma(sp, lnb_bc[:], rows_d[:, R_LNB:R_LNB + 512], d_rowc, writes=[b_rowc])
        P.dma(sp, bs_full[:], rows_d[:, R_BS:R_BS + 512], d_rowc, writes=[b_rowc])
        P.dma(sp, stage[0:1, 0:256], rows_d[0:1, R_BINA:R_BINA + 256], d_stage, writes=[b_stage])
        for q in range(4):
            if q > 0:
                P.dma(sp, stage[0:1, 0:256], rows_d[0:1, R_BINA + 256 * q:R_BINA + 256 * (q + 1)], d_stage, writes=[b_stage])
            P.emit(dve, lambda e, q=q: e.tensor_copy(out=binhi[0:1, 256 * q:256 * (q + 1)], in_=stage[0:1, 0:256]),
                   reads=[b_stage], writes=[b_bin])
        P.dma(sp, x_sb[:, 0:4, :], x_d[0:ST, :].rearrange("(i p) d -> p i d", p=128), d_x[0], writes=b_x[0:4])
        xh_t = x_sb[0:HALO, 4, :]; b_xh = b_x[4]
        P.dma(sp, xh_t, xh_d[:, :], d_xh, writes=[b_xh])
        wspf = x_sb[:, 5, :].rearrange("p (h t) -> p h t", h=8); b_wspf = b_x[5]
        P.dma(sp, wspf, wsp_d[:, :, :], d_wspf, writes=[b_wspf])
        P.emit(pool, lambda e: e.affine_select(out=wspf, in_=wspf, pattern=[[0, 8], [1, 128]],
                                               compare_op=ALU.is_ge, fill=0.0, base=0, channel_multiplier=-1),
               reads=[b_wspf], writes=[b_wspf])
        P.emit(dve, lambda e: e.tensor_copy(out=wct[:], in_=wspf), reads=[b_wspf], writes=[b_wct])
        bmat = sb("bmat", [128, 128]); b_bmat = Buf()
        P.emit(dve, lambda e: e.tensor_tensor(out=bmat[:], in0=identf[:], in1=jb[:], op=ALU.subtract),
               reads=[b_identf, b_jb], writes=[b_bmat])
        b_convLc = [Buf() for _ in range(4)]

        def gen_convL(which):
            for cc in ((0, 1, 2) if which == 'dve' else (3,)):
                for k in range(31):
                    col = cols[:, C_CW + cc * 31 + k:C_CW + cc * 31 + k + 1]
                    if which == 'dve':
                        P.emit(dve, lambda e, cc=cc, k=k, col=col: e.tensor_scalar(
                            out=convL[:, cc, k, :], in0=bmat[:], scalar1=col, scalar2=None, op0=ALU.mult),
                            reads=[b_bmat, b_cols], writes=[b_convLc[cc]])
                    else:
                        P.emit(act, lambda e, cc=cc, k=k, col=col: e.activation(
                            out=convL[:, cc, k, :], in_=bmat[:], func=AF.Identity, scale=col),
                            reads=[b_bmat, b_cols], writes=[b_convLc[cc]])

        cbb = sb("cbb", [128, 4], BF16); b_cbb = Buf()
        P.emit(dve, lambda e: e.tensor_copy(out=cbb[:], in_=cols[:, C_CB:C_CB + 4]), reads=[b_cols], writes=[b_cbb])
        bk, bb = palloc()
        P.emit(pe, lambda e: e.matmul(bk[:, 0:4], lhsT=jb[:], rhs=cbb[:], start=True, stop=True),
               reads=[b_jb, b_cbb], writes=[bb])
        P.emit(dve, lambda e: e.tensor_tensor(out=ccols[:, CCBP:CCBP + 4], in0=cols[:, C_CB:C_CB + 4], in1=bk[:, 0:4],
                                              op=ALU.subtract), reads=[b_cols, bb], writes=[b_ccols])
        caf = sb("caf", [128, 8]); b_caf = Buf()
        cab = sb("cab", [128, 8], BF16); b_cab = Buf()
        P.emit(act, lambda e: e.activation(out=caf[:], in_=cols[:, C_CC:C_CC + 8], func=AF.Silu),
               reads=[b_cols], writes=[b_caf])
        P.emit(dve, lambda e: e.tensor_copy(out=cab[:], in_=caf[:]), reads=[b_caf], writes=[b_cab])
        for k in range(8):
            P.emit(dve, lambda e, k=k: e.tensor_copy(out=caH[:, k, :], in_=cab[:, k:k + 1].to_broadcast([128, 128])),
                   reads=[b_cab], writes=[b_caH])

        gate1_bc = x_sb[:, 6, :]; b_g1bc = b_x[6]
        wstage = x_sb[:, 7, :]; b_wstage = b_x[7]
        ostage = sb("ostage", [128, 512]); b_ostage = Buf()

        ring_views = []
        ring_bufs = []
        for j in range(4):
            v = x_sb[:, 8 + 2 * j:10 + 2 * j, :].rearrange("p a d -> p (a d)").bitcast(BF16)
            ring_views.append(v.rearrange("p (k n) -> p k n", k=8))
            ring_bufs.append([b_x[8 + 2 * j], b_x[9 + 2 * j]])
        ada_dma_next = [0]

        def ada_slot(b):
            return b % 4 if b < 12 else 2 + (b % 2)

        def adaln_dma(upto):
            while ada_dma_next[0] <= min(upto, 15):
                b = ada_dma_next[0]
                if b < 12:
                    src = adaw_d[:, b * 512:(b + 1) * 512]
                else:
                    src = adafw_d[:, (b - 12) * 512:(b - 11) * 512]
                sl_ = ada_slot(b)
                P.dma(pool, ring_views[sl_], src.rearrange("(k p) n -> p k n", p=128), d_ring[sl_],
                      writes=ring_bufs[sl_])
                ada_dma_next[0] += 1

        def adaln_block(b):
            kind = b // 2
            q = b % 2
            adaln_dma(b)
            ring = ring_views[ada_slot(b)]
            rb = ring_bufs[ada_slot(b)]
            bk, bb = palloc()
            for k in range(8):
                P.emit(pe, lambda e, k=k: e.matmul(bk[:], lhsT=caH[:, k, :], rhs=ring[:, k, :],
                                                   start=(k == 0), stop=(k == 7)),
                       reads=[b_caH] + rb, writes=[bb], inc=(k == 7))
            adaln_dma(min(b + 3, 11) if b + 1 < 12 else b + 1)
            if kind in (0, 1, 3, 4):
                for h in range(4):
                    ch = q * 4 + h
                    P.emit(dve, lambda e, h=h: e.tensor_tensor(out=tmpR[:, 0:128], in0=bk[:, h * 128:(h + 1) * 128],
                                                              in1=identf[:], op=ALU.mult),
                           reads=[bb, b_identf], writes=[b_tmpR])
                    P.emit(dve, lambda e: e.tensor_reduce(out=tmpR[:, 128:129], in_=tmpR[:, 0:128], axis=AX.X, op=ALU.add),
                           reads=[b_tmpR], writes=[b_tmpR])
                    abc = C_AB + {0: 0, 1: 8, 3: 16, 4: 24}[kind] + ch
                    if kind in (0, 3):
                        dst = (CS1 if kind == 0 else CS2) + ch
                        P.emit(dve, lambda e, dst=dst, abc=abc: e.tensor_tensor(
                            out=ccols[:, dst:dst + 1], in0=tmpR[:, 128:129], in1=cols[:, abc:abc + 1], op=ALU.add),
                            reads=[b_tmpR, b_cols], writes=[b_ccols])
                    else:
                        dst = (CA1 if kind == 1 else CA2) + ch
                        gcol = (C_G1 if kind == 1 else C_G2) + ch
                        P.emit(dve, lambda e, abc=abc: e.tensor_scalar(
                            out=tmpR[:, 129:130], in0=tmpR[:, 128:129], scalar1=cols[:, abc:abc + 1], scalar2=1.0,
                            op0=ALU.add, op1=ALU.add), reads=[b_tmpR, b_cols], writes=[b_tmpR])
                        P.emit(dve, lambda e, dst=dst, gcol=gcol: e.tensor_tensor(
                            out=ccols[:, dst:dst + 1], in0=tmpR[:, 129:130], in1=cols[:, gcol:gcol + 1], op=ALU.mult),
                            reads=[b_tmpR, b_cols], writes=[b_ccols])
            else:
                roff = {2: R_ABG1, 5: R_ABG2, 6: R_ABSF, 7: R_ABSCF}[kind] + q * 512
                P.dma(sp, stage[:], rows_d[:, roff:roff + 512], d_stage, writes=[b_stage])
                if kind == 2:
                    P.emit(dve, lambda e: e.tensor_tensor(out=gate1_bc[:, q * 512:(q + 1) * 512], in0=bk[:],
                                                          in1=stage[:], op=ALU.add),
                           reads=[bb, b_stage], writes=[b_g1bc])
                else:
                    P.emit(dve, lambda e: e.tensor_tensor(out=ostage[:], in0=bk[:], in1=stage[:], op=ALU.add),
                           reads=[bb, b_stage], writes=[b_ostage])
                    if kind == 7:
                        P.dma(sp, stage[:], rows_d[:, R_GF + q * 512:R_GF + (q + 1) * 512], d_stage, writes=[b_stage])
                        P.emit(dve, lambda e: e.scalar_tensor_tensor(out=ostage[:], in0=ostage[:], scalar=1.0, in1=stage[:],
                                                                     op0=ALU.add, op1=ALU.mult),
                               reads=[b_ostage, b_stage], writes=[b_ostage])
                    so = {5: 0, 6: D, 7: 2 * D}[kind] + q * 512
                    P.dma(sp, scr_d[:, so:so + 512], ostage[:], d_ostage, reads=[b_ostage])

        adaln_dma(3)
        d_w2 = P.dsem("d_w2")
        b_winA = Buf()
        P.dma(pool, w_in_sb[:, :, D:2 * D], win_d[:, D:2 * D].rearrange("(k p) n -> p k n", p=128), d_w, writes=[b_win])
        P.dma(pool, w_in_sb[:, :, 0:D], win_d[:, 0:D].rearrange("(k p) n -> p k n", p=128), d_w2, writes=[b_winA])
        stop_at(1)
        for b in range(4):
            adaln_block(b)
        stop_at(2)

        def wout_prep():
            for k in range(8):
                P.dma(sp, wstage, wout_d[k * 128:(k + 1) * 128, :], d_wstage, writes=[b_wstage])
                P.emit(pool, lambda e, k=k: e.tensor_tensor(out=wstage, in0=wstage, in1=gate1_bc, op=ALU.mult),
                       reads=[b_wstage, b_g1bc], writes=[b_wstage])
                gcol = (C_GA + k) if k < 4 else (C_GB + k - 4)
                P.emit(pool, lambda e, k=k, gcol=gcol: e.tensor_tensor(out=w_out_sb[:, k, :], in0=wstage,
                                                                      in1=cols[:, gcol:gcol + 1].to_broadcast([128, D]),
                                                                      op=ALU.mult),
                       reads=[b_wstage, b_cols], writes=[b_wout[k]])

        def rstd_from(sum_ap, out_ap, n, dim, rbufs, wbufs):
            P.emit(act, lambda e: e.activation(out=out_ap, in_=sum_ap, func=AF.Ln, bias=EPS, scale=1.0 / dim),
                   reads=rbufs, writes=wbufs)
            P.emit(act, lambda e: e.activation(out=out_ap, in_=out_ap, func=AF.Exp, scale=-0.5),
                   reads=wbufs, writes=wbufs)

        def norm_transpose(src_tiles, src_bufs, ntok, acol, scol, sbuf_stat, dst, dst_bufs, tw, phase=0):
            nt = len(src_tiles)
            ss = stat[0:tw, 0:nt]
            rs = stat[0:tw, 8:8 + nt]
            def do_xn(i):
                t, tb = src_tiles[i], src_bufs[i]
                xb = i % 2
                P.emit(dve, lambda e: e.tensor_scalar(out=xn[0:tw, xb, :], in0=t,
                                                      scalar1=stat[0:tw, 8 + i:9 + i], scalar2=None,
                                                      op0=ALU.mult),
                       reads=[tb, sbuf_stat], writes=[b_xn[xb]])

            if phase in (0, 1):
                for i, (t, tb) in enumerate(zip(src_tiles, src_bufs)):
                    P.emit(act, lambda e, t=t, i=i: e.activation(out=junkx[0:tw, :], in_=t, func=AF.Square,
                                                                 accum_out=stat[0:tw, i:i + 1]),
                           reads=[tb], writes=[b_junkx, sbuf_stat])
                rstd_from(ss, rs, nt, D, [sbuf_stat], [sbuf_stat])
                for i in range(min(2, nt)):
                    do_xn(i)
            if phase == 1:
                return
            pbs = [palloc() for _ in range(4)]
            for i, (t, tb) in enumerate(zip(src_tiles, src_bufs)):
                xb = i % 2
                if i >= 2:
                    do_xn(i)
                for k in range(8):
                    bk, bb = pbs[k // 2]
                    pv = bk[:].bitcast(BF16)
                    c0 = (k % 2) * 512 + i * tw
                    P.emit(pe, lambda e, pv=pv, c0=c0, k=k, xb=xb: e.transpose(
                        out=pv[:, c0:c0 + tw], in_=xn[0:tw, xb, k * 128:(k + 1) * 128], identity=identb[0:tw, 0:tw]),
                        reads=[b_xn[xb], b_identb], writes=[bb], inc=(k == 7))
            if tw == 128: stop_at(32)
            n = nt * tw
            for k in range(8):
                bk, bb = pbs[k // 2]
                pv = bk[:].bitcast(BF16)
                c0 = (k % 2) * 512
                eng = act if (k // 2) % 2 == 0 else dve
                if os.environ.get('KEV') == 'act': eng = act
                if os.environ.get('KEV') == 'dve': eng = dve
                if eng is act:
                    P.emit(act, lambda e, pv=pv, c0=c0, k=k: e.activation(
                        out=dst[:, k, 0:n], in_=pv[:, c0:c0 + n], func=AF.Identity,
                        bias=ccols[:, scol + k:scol + k + 1], scale=ccols[:, acol + k:acol + k + 1]),
                        reads=[bb, b_ccols], writes=[dst_bufs[k]])
                else:
                    P.emit(dve, lambda e, pv=pv, c0=c0, k=k: e.tensor_scalar(
                        out=dst[:, k, 0:n], in0=pv[:, c0:c0 + n], scalar1=ccols[:, acol + k:acol + k + 1],
                        scalar2=ccols[:, scol + k:scol + k + 1], op0=ALU.mult, op1=ALU.add),
                        reads=[bb, b_ccols], writes=[dst_bufs[k]])

        def b_branch_y(n, col0, mask):
            for cc in range(4):
                bv, bbv = palloc()
                bg, bbg = palloc()
                for (bk, bb, coff) in ((bv, bbv, D + cc * 128), (bg, bbg, D + 512 + cc * 128)):
                    for k in range(8):
                        P.emit(pe, lambda e, bk=bk, k=k, coff=coff: e.matmul(
                            bk[:, 0:n], lhsT=w_in_sb[:, k, coff:coff + 128], rhs=actT[:, k, 0:n],
                            start=(k == 0), stop=(k == 7)),
                            reads=[b_win, b_actT[k]], writes=[bb], inc=(k == 7))
                P.emit(act, lambda e, bg=bg, cc=cc: e.activation(out=sig[:, 0:n], in_=bg[:, 0:n], func=AF.Sigmoid,
                                                                bias=cols[:, C_BB + 4 + cc:C_BB + 5 + cc]),
                       reads=[bbg, b_cols], writes=[b_sig])
                if mask:
                    P.emit(dve, lambda e: e.tensor_scalar(out=sig[:, 0:n], in0=sig[:, 0:n], scalar1=cols[:, C_HM:C_HM + 1],
                                                          scalar2=None, op0=ALU.mult),
                           reads=[b_sig, b_cols], writes=[b_sig])
                P.emit(dve, lambda e, bv=bv, cc=cc: e.scalar_tensor_tensor(
                    out=ybuf[:, cc, col0:col0 + n], in0=bv[:, 0:n], scalar=cols[:, C_BB + cc:C_BB + cc + 1],
                    in1=sig[:, 0:n], op0=ALU.add, op1=ALU.mult),
                    reads=[bbv, b_cols, b_sig], writes=[b_ybuf[cc]])

        stop_at(3)
        norm_transpose([xh_t], [b_xh], HALO, CA1, CS1, b_stat[1], actT, b_actT, HALO)
        b_branch_y(HALO, 0, True)

        stop_at(4)
        ada_next = [4]

        def ada_fill(n):
            for _ in range(n):
                if ada_next[0] < 16:
                    adaln_block(ada_next[0])
                    ada_next[0] += 1

        def prefetch_x(sn):
            P.dma(sp, x_sb[:, 4 * sn:4 * (sn + 1), :],
                  x_d[sn * ST:(sn + 1) * ST, :].rearrange("(i p) d -> p i d", p=128), d_x[sn],
                  writes=b_x[4 * sn:4 * (sn + 1)])

        def front_tiles(sn):
            return [x_sb[:, 4 * sn + i, :] for i in range(4)], b_x[4 * sn:4 * sn + 4]

        t0_, tb0_ = front_tiles(0)
        norm_transpose(t0_, tb0_, 128, CA1, CS1, b_stat[1], actT, b_actT, 128, phase=1)
        for s in range(NST):
            if 1 <= s and s + 1 < NST:
                assert ada_next[0] >= (12 if s == 1 else 16)
                prefetch_x(s + 1)
            tiles, tbufs = front_tiles(s)
            norm_transpose(tiles, tbufs, 128, CA1, CS1, b_stat[1], actT, b_actT, 128, phase=2)
            if s == 0:
                gen_convL('dve')
            for i in range(4):
                for half, (dst, dbuf) in enumerate(((gu4, b_gu[i]), (gv4, b_gv[i]))):
                    bk, bb = palloc()
                    for k in range(8):
                        P.emit(pe, lambda e, bk=bk, k=k, i=i, half=half: e.matmul(
                            bk[:], lhsT=actT[:, k, i * 128:(i + 1) * 128], rhs=w_in_sb[:, k, half * 512:(half + 1) * 512],
                            start=(k == 0), stop=False), reads=[b_actT[k], b_winA], writes=[bb], inc=False)
                    P.emit(pe, lambda e, bk=bk, half=half: e.matmul(
                        bk[:], lhsT=ones1[0:1, :], rhs=binhi[0:1, half * 512:(half + 1) * 512], start=False, stop=True),
                        reads=[b_ones1, b_bin], writes=[bb])
                    P.emit(act, lambda e, bk=bk, dst=dst, i=i: e.activation(out=dst[:, i, :], in_=bk[:], func=AF.Gelu),
                           reads=[bb], writes=[dbuf])
                P.emit(dve, lambda e, i=i: e.bn_stats(out=stat[:, 16 + 6 * i:22 + 6 * i], in_=gv4[:, i, :]),
                       reads=[b_gv[i]], writes=[b_stat[2]])
                P.emit(dve, lambda e, i=i: e.bn_aggr(out=stat[:, 40 + 2 * i:42 + 2 * i], in_=stat[:, 16 + 6 * i:22 + 6 * i]),
                       reads=[b_stat[2]], writes=[b_stat[3]])
            if s == 0:
                gen_convL('act')
                ada_fill(2)
                wout_prep()
                prefetch_x(1)
                ada_fill(1)
            if s == 1:
                ada_fill(1)
            b_branch_y(ST, HALO, False)
            mv = stat[:, 40:48].rearrange("p (i t) -> p i t", t=2)
            P.emit(act, lambda e: e.activation(out=stat[:, 48:52], in_=mv[:, :, 1], func=AF.Ln, bias=EPS, scale=1.0),
                   reads=[b_stat[3]], writes=[b_stat[4]])
            P.emit(act, lambda e: e.activation(out=stat[:, 48:52], in_=stat[:, 48:52], func=AF.Exp, scale=-0.5),
                   reads=[b_stat[4]], writes=[b_stat[4]])
            if s == 0:
                ada_fill(2)
            if s == 1:
                ada_fill(1)

            bm, bbm = banks[7], bank_bufs[7]
            dbank = {}
            sbank = {}
            ptA = []

            def conv(cc):
                bi_ = (0, 1, 4)[cc % 3]
                bd, bbd = banks[bi_], bank_bufs[bi_]
                dbank[cc] = (bd, bbd)
                for k in range(31):
                    P.emit(pe, lambda e, k=k: e.matmul(
                        bd[:], lhsT=convL[:, cc, k, :], rhs=ybuf[:, cc, HALO - 30 + k:HALO - 30 + k + ST],
                        start=(k == 0), stop=(k == 30)), reads=[b_convLc[cc], b_ybuf[cc]], writes=[bbd], inc=(k == 30))
                P.emit(act, lambda e: e.activation(out=dsq2[:, cc % 2, :], in_=bd[:], func=AF.Square,
                                                   bias=ccols[:, CCBP + cc:CCBP + cc + 1]),
                       reads=[bbd, b_ccols], writes=[b_dsq2[cc % 2]])
                P.emit(dve, lambda e: e.tensor_copy(out=ybuf[:, cc, 0:HALO], in_=ybuf[:, cc, ST:ST + HALO]),
                       reads=[b_ybuf[cc]], writes=[b_ybuf[cc]])

            def gn_var(cc):
                bd, bbd = dbank[cc]
                bvv, bbvv = banks[2], bank_bufs[2]
                P.emit(pe, lambda e: e.matmul(bvv[:], lhsT=jb[:], rhs=dsq2[:, cc % 2, :], start=True, stop=True),
                       reads=[b_jb, b_dsq2[cc % 2]], writes=[bbvv])
                P.emit(act, lambda e: e.activation(out=lnv[:], in_=bvv[:], func=AF.Ln, bias=EPS, scale=1.0),
                       reads=[bbvv], writes=[b_lnv])
                P.emit(act, lambda e: e.activation(out=lnv[:], in_=lnv[:], func=AF.Exp, scale=-0.5),
                       reads=[b_lnv], writes=[b_lnv])
                P.emit(dve, lambda e: e.scalar_tensor_tensor(out=cnb[:], in0=bd[:], scalar=ccols[:, CCBP + cc:CCBP + cc + 1],
                                                             in1=lnv[:], op0=ALU.add, op1=ALU.mult),
                       reads=[bbd, b_ccols, b_lnv], writes=[b_cnb])
                P.emit(act, lambda e: e.activation(out=actT[:, 4 + cc, :], in_=cnb[:], func=AF.Silu,
                                                   bias=cols[:, C_GNB + cc:C_GNB + cc + 1],
                                                   scale=cols[:, C_GNG + cc:C_GNG + cc + 1]),
                       reads=[b_cnb, b_cols], writes=[b_actT[4 + cc]])
                P.emit(act, lambda e: e.activation(out=ybsq, in_=actT[:, 4 + cc, :], func=AF.Square),
                       reads=[b_actT[4 + cc]], writes=[b_ybsq])

            def gn_msq(cc):
                for i in range(4):
                    P.emit(pe, lambda e, i=i: e.matmul(bm[:, i * 4 + cc:i * 4 + cc + 1], lhsT=ybsq[:, i * 128:(i + 1) * 128],
                                                       rhs=onesm[:, 0:1], start=True, stop=True),
                           reads=[b_onesm, b_ybsq], writes=[bbm], inc=(i == 3))

            def a_pre(i):
                P.emit(dve, lambda e: e.tensor_scalar(out=vtmp[:], in0=gv4[:, i, :], scalar1=stat[:, 40 + 2 * i:41 + 2 * i],
                                                      scalar2=stat[:, 48 + i:49 + i], op0=ALU.subtract, op1=ALU.mult),
                       reads=[b_gv[i], b_stat[3], b_stat[4]], writes=[b_vtmp])
                P.emit(dve, lambda e: e.tensor_tensor(out=vtmp[:], in0=vtmp[:], in1=lng_bc[:], op=ALU.mult),
                       reads=[b_vtmp, b_rowc], writes=[b_vtmp])
                P.emit(dve, lambda e: e.tensor_tensor(out=vln[:], in0=vtmp[:], in1=lnb_bc[:], op=ALU.add),
                       reads=[b_vtmp, b_rowc], writes=[b_vln])
                bk, bb = banks[3], bank_bufs[3]
                sbank[i] = (bk, bb)
                for h in range(8):
                    P.emit(pe, lambda e, h=h: e.matmul(bk[:, h * 64:(h + 1) * 64], lhsT=wct[:, h, :],
                                                       rhs=vln[:, h * 64:(h + 1) * 64], start=True, stop=True),
                           reads=[b_wct, b_vln], writes=[bb], inc=(h == 7))

            def a_post(i):
                bk, bb = sbank[i]
                yb_ = i % 2
                P.emit(dve, lambda e: e.tensor_tensor(out=ya[:], in0=bk[:], in1=bs_full[:], op=ALU.add),
                       reads=[bb, b_rowc], writes=[b_ya])
                P.emit(dve, lambda e: e.tensor_tensor(out=yan2[:, yb_, :], in0=ya[:], in1=gu4[:, i, :], op=ALU.mult),
                       reads=[b_ya, b_gu[i]], writes=[b_yan2[yb_]])
                P.emit(act, lambda e: e.activation(out=junk[:, 0:512], in_=yan2[:, yb_, :], func=AF.Square,
                                                   accum_out=stat[:, 52 + i:53 + i]),
                       reads=[b_yan2[yb_]], writes=[b_dsq, b_stat[5]])
                if i == 0:
                    ptA.extend([(banks[5], bank_bufs[5]), (banks[6], bank_bufs[6])])
                for c in range(4):
                    bk2, bb2 = ptA[c // 2]
                    pv = bk2[:].bitcast(BF16)
                    c0 = (c % 2) * 512 + i * 128
                    P.emit(pe, lambda e, pv=pv, c0=c0, c=c: e.transpose(out=pv[:, c0:c0 + 128],
                                                                       in_=yan2[:, yb_, c * 128:(c + 1) * 128],
                                                                       identity=identb[:]),
                           reads=[b_yan2[yb_], b_identb], writes=[bb2], inc=(c == 3))

            conv(0)
            a_pre(0)
            conv(1)
            a_post(0)
            a_pre(1)
            gn_var(0)
            conv(2)
            a_post(1)
            a_pre(2)
            gn_msq(0)
            gn_var(1)
            conv(3)
            a_post(2)
            a_pre(3)
            gn_msq(1)
            gn_var(2)
            a_post(3)
            for c in range(4):
                bk2, bb2 = ptA[c // 2]
                pv = bk2[:].bitcast(BF16)
                c0 = (c % 2) * 512
                P.emit(dve, lambda e, pv=pv, c0=c0, c=c: e.tensor_copy(out=actT[:, c, :], in_=pv[:, c0:c0 + 512]),
                       reads=[bb2], writes=[b_actT[c]])
            gn_msq(2)
            gn_var(3)
            if s == 0:
                ada_fill(3)
            if s == 1:
                ada_fill(2)
                assert ada_next[0] >= 16
                for q_ in range(6):
                    c0_, w_ = (q_ * 512, 512) if q_ < 5 else (2560, 256)
                    for g_ in range(2):
                        P.dma(pool, scr_wi[q_, :, :, g_, 0:w_],
                              wfi_d[:, g_ * DFF + c0_:g_ * DFF + c0_ + w_].rearrange("(k p) n -> p k n", p=128), d_cast)
                for h_ in range(2):
                    P.dma(pool, scr_wo[:, 11 * h_:11 * (h_ + 1), :],
                          wfo_d[11 * h_ * 128:11 * (h_ + 1) * 128, :].rearrange("(j p) n -> p j n", p=128), d_cast)
            if s + 1 < NST:
                tn, tbn = front_tiles(s + 1)
                norm_transpose(tn, tbn, 128, CA1, CS1, b_stat[1], actT, b_actT, 128, phase=1)
            rstd_from(stat[:, 52:56], stat[:, 56:60], 4, 512, [b_stat[5]], [b_stat[6]])
            for i in range(4):
                for hf in range(2):
                    pa, bpa = palloc()
                    pb_, bpb = palloc()
                    for (bk, bb, k0) in ((pa, bpa, 0), (pb_, bpb, 4)):
                        for k in range(k0, k0 + 4):
                            P.emit(pe, lambda e, bk=bk, k=k, k0=k0: e.matmul(
                                bk[:], lhsT=actT[:, k, i * 128:(i + 1) * 128], rhs=w_out_sb[:, k, hf * 512:(hf + 1) * 512],
                                start=(k == k0), stop=(k == k0 + 3)), reads=[b_actT[k], b_wout[k]], writes=[bb],
                                inc=(k == k0 + 3))
                    if i == 0 and hf == 0:
                        gn_msq(3)
                        P.emit(dve, lambda e: e.tensor_reduce(out=stat[:, 60:64],
                                                              in_=bm[:, 0:16].rearrange("p (i c) -> p i c", c=4),
                                                              axis=AX.X, op=ALU.add), reads=[bbm], writes=[b_stat[9]])
                        rstd_from(stat[:, 60:64], stat[:, 60:64], 4, 1, [b_stat[9]], [b_stat[9]])
                    xt = x_sb[:, 4 * s + i, hf * 512:(hf + 1) * 512]
                    P.emit(dve, lambda e: e.scalar_tensor_tensor(out=xt, in0=pa[:], scalar=stat[:, 56 + i:57 + i], in1=xt,
                                                                 op0=ALU.mult, op1=ALU.add),
                           reads=[bpa, b_stat[6], b_x[4 * s + i]], writes=[b_x[4 * s + i]])
                    P.emit(dve, lambda e: e.scalar_tensor_tensor(out=xt, in0=pb_[:], scalar=stat[:, 60 + i:61 + i], in1=xt,
                                                                 op0=ALU.mult, op1=ALU.add),
                           reads=[bpb, b_stat[9], b_x[4 * s + i]], writes=[b_x[4 * s + i]])
            stop_at(5 + s)
        assert ada_next[0] >= 16
        stop_at(9)
        rr_n[0] = 8
        P.barrier()
        nA = len(ctxs)
        keep = 8 + 9
        for cm in reversed(ctxs[keep:]):
            cm.__exit__(None, None, None)
        del ctxs[keep:]

        gate2_bc = sb("gate2_bc", [128, D]); af_bc = sb("af_bc", [128, D]); sf_bc = sb("sf_bc", [128, D])
        b_condB = Buf()
        h2T = sb("h2T", [128, 8, ST], BF16); b_h2T = [Buf() for _ in range(8)]
        aT = sb("aT", [128, 22, ST], BF16); b_aT = [Buf() for _ in range(22)]
        wo_sb = sb("wo_sb", [128, 22, D], BF16); b_wo = Buf()
        wi_ring = sb("wi_ring", [128, 2, 8, 2, 512], BF16); b_wi = [Buf() for _ in range(2)]
        xn2 = sb("xn2", [128, 2, D], BF16); b_xn2 = [Buf(), Buf()]
        junk2 = sb("junk2", [128, D], BF16); b_junk2 = Buf()
        sg = sb("sg", [128, 2, 512]); b_sg = [Buf(), Buf()]
        ost = sb("ost", [128, 2, D]); b_ost = [Buf(), Buf()]
        d_cb = P.dsem("d_cb")
        d_wo = P.dsem("d_wo")
        d_wi = [P.dsem(f"d_wi{i}") for i in range(2)]

        P.dma(sp, gate2_bc[:], scr_d[:, 0:D], d_cb, writes=[b_condB])
        P.dma(sp, sf_bc[:], scr_d[:, D:2 * D], d_cb, writes=[b_condB])
        P.dma(sp, af_bc[:], scr_d[:, 2 * D:3 * D], d_cb, writes=[b_condB])

        NLD = 6
        NL = NST * NLD
        issued = [0]
        wfi_v = wfi_d.rearrange("(k p) (g c) -> p k g c", p=128, g=2)
        wo_loaded = [False]

        def load_cols(q):
            return (q * 512, 512) if q < 5 else (2560, 256)

        def wi_ensure(upto):
            while issued[0] <= min(upto, NL - 1):
                g = issued[0]
                q_ = g % NLD
                c0, w = load_cols(q_)
                sl = g % 2
                P.dma(sp, wi_ring[:, sl, :, :, 0:w], scr_wi[q_, :, :, :, 0:w], d_wi[sl], writes=[b_wi[sl]])
                issued[0] += 1
                if g == 1 and not wo_loaded[0]:
                    wo_loaded[0] = True
                    P.dma(sp, wo_sb[:], scr_wo[:, :, :], d_wo, writes=[b_wo])

        wi_ensure(1)

        def norm_transpose_b(s, phase=0):
            ss = stat[:, 0:4]
            rs = stat[:, 8:12]

            def do_xn2(i):
                xb = i % 2
                P.emit(dve, lambda e: e.tensor_scalar(out=xn2[:, xb, :], in0=x_sb[:, 4 * s + i, :],
                                                      scalar1=stat[:, 8 + i:9 + i], scalar2=None, op0=ALU.mult),
                       reads=[b_x[4 * s + i], b_stat[1]], writes=[b_xn2[xb]])

            for i in (range(4) if phase in (0, 1) else ()):
                P.emit(act, lambda e, i=i: e.activation(out=junk2[:], in_=x_sb[:, 4 * s + i, :], func=AF.Square,
                                                        accum_out=stat[:, i:i + 1]),
                       reads=[b_x[4 * s + i]], writes=[b_junk2, b_stat[1]])
            if phase in (0, 1):
                P.emit(act, lambda e: e.activation(out=rs, in_=ss, func=AF.Ln, bias=EPS, scale=1.0 / D),
                       reads=[b_stat[1]], writes=[b_stat[1]])
                P.emit(act, lambda e: e.activation(out=rs, in_=rs, func=AF.Exp, scale=-0.5),
                       reads=[b_stat[1]], writes=[b_stat[1]])
                do_xn2(0)
                do_xn2(1)
            if phase == 1:
                return
            pbs = [palloc() for _ in range(4)]
            for i in range(4):
                xb = i % 2
                if i >= 2:
                    do_xn2(i)
                for k in range(8):
                    bk, bb = pbs[k // 2]
                    pv = bk[:].bitcast(BF16)
                    c0 = (k % 2) * 512 + i * 128
                    P.emit(pe, lambda e, pv=pv, c0=c0, k=k, xb=xb: e.transpose(
                        out=pv[:, c0:c0 + 128], in_=xn2[:, xb, k * 128:(k + 1) * 128], identity=identb[:]),
                        reads=[b_xn2[xb], b_identb], writes=[bb], inc=(k == 7))
            for k in range(8):
                bk, bb = pbs[k // 2]
                pv = bk[:].bitcast(BF16)
                c0 = (k % 2) * 512
                if (k // 2) % 2 == 0:
                    P.emit(act, lambda e, pv=pv, c0=c0, k=k: e.activation(
                        out=h2T[:, k, :], in_=pv[:, c0:c0 + 512], func=AF.Identity,
                        bias=ccols[:, CS2 + k:CS2 + k + 1], scale=ccols[:, CA2 + k:CA2 + k + 1]),
                        reads=[bb, b_ccols], writes=[b_h2T[k]])
                else:
                    P.emit(dve, lambda e, pv=pv, c0=c0, k=k: e.tensor_scalar(
                        out=h2T[:, k, :], in0=pv[:, c0:c0 + 512], scalar1=ccols[:, CA2 + k:CA2 + k + 1],
                        scalar2=ccols[:, CS2 + k:CS2 + k + 1], op0=ALU.mult, op1=ALU.add),
                        reads=[bb, b_ccols], writes=[b_h2T[k]])

        oc = [0]
        norm_transpose_b(0)
        for s in range(NST):
            for q in range(NLD):
                g = s * NLD + q
                wi_ensure(g + 1)
                if q == 2 and s + 1 < NST:
                    norm_transpose_b(s + 1, phase=1)
                sl = g % 2
                c0, w = load_cols(q)
                for jj in range(w // 128):
                    j = c0 // 128 + jj
                    bg, bbg = palloc()
                    bu, bbu = palloc()
                    for (bk, bb, g_) in ((bg, bbg, 0), (bu, bbu, 1)):
                        for k in range(8):
                            P.emit(pe, lambda e, bk=bk, k=k, g_=g_, sl=sl, jj=jj: e.matmul(
                                bk[:], lhsT=wi_ring[:, sl, k, g_, jj * 128:(jj + 1) * 128], rhs=h2T[:, k, :],
                                start=(k == 0), stop=(k == 7)), reads=[b_wi[sl], b_h2T[k]], writes=[bb], inc=(k == 7))
                    sb_ = j % 2
                    P.emit(act, lambda e, bg=bg, sb_=sb_: e.activation(out=sg[:, sb_, :], in_=bg[:], func=AF.Silu),
                           reads=[bbg], writes=[b_sg[sb_]])
                    P.emit(dve, lambda e, bu=bu, sb_=sb_, j=j: e.tensor_tensor(out=aT[:, j, :], in0=bu[:], in1=sg[:, sb_, :],
                                                                             op=ALU.mult),
                           reads=[bbu, b_sg[sb_]], writes=[b_aT[j]])
            if s + 1 < NST:
                norm_transpose_b(s + 1, phase=2)
            for i in range(4):
                ti = 4 * s + i
                pbk = []
                for hf in range(2):
                    bk, bb = palloc()
                    pbk.append((bk, bb))
                    for j in range(22):
                        P.emit(pe, lambda e, bk=bk, j=j, i=i, hf=hf: e.matmul(
                            bk[:], lhsT=aT[:, j, i * 128:(i + 1) * 128], rhs=wo_sb[:, j, hf * 512:(hf + 1) * 512],
                            start=(j == 0), stop=(j == 21)), reads=[b_aT[j], b_wo], writes=[bb], inc=(j == 21))
                ob = oc[0] % 2
                oc[0] += 1
                for hf in range(2):
                    bk, bb = pbk[hf]
                    sl_ = slice(hf * 512, (hf + 1) * 512)
                    P.emit(dve, lambda e, bk=bk, sl_=sl_, ob=ob: e.tensor_tensor(out=ost[:, ob, sl_], in0=bk[:],
                                                                                in1=gate2_bc[:, sl_], op=ALU.mult),
                           reads=[bb, b_condB], writes=[b_ost[ob]])
                P.emit(dve, lambda e, ti=ti, ob=ob: e.tensor_tensor(out=x_sb[:, ti, :], in0=x_sb[:, ti, :], in1=ost[:, ob, :],
                                                                   op=ALU.add),
                       reads=[b_x[ti], b_ost[ob]], writes=[b_x[ti]])
                P.emit(act, lambda e, ti=ti, i=i: e.activation(out=junk2[:], in_=x_sb[:, ti, :], func=AF.Square,
                                                               accum_out=stat[:, 16 + i:17 + i]),
                       reads=[b_x[ti]], writes=[b_junk2, b_stat[7]])
                P.emit(act, lambda e, i=i: e.activation(out=stat[:, 24 + i:25 + i], in_=stat[:, 16 + i:17 + i], func=AF.Ln,
                                                        bias=EPS, scale=1.0 / D), reads=[b_stat[7]], writes=[b_stat[8]])
                P.emit(act, lambda e, i=i: e.activation(out=stat[:, 24 + i:25 + i], in_=stat[:, 24 + i:25 + i], func=AF.Exp,
                                                        scale=-0.5), reads=[b_stat[8]], writes=[b_stat[8]])
                P.emit(dve, lambda e, ti=ti, ob=ob, i=i: e.scalar_tensor_tensor(
                    out=ost[:, ob, :], in0=x_sb[:, ti, :], scalar=stat[:, 24 + i:25 + i], in1=af_bc[:],
                    op0=ALU.mult, op1=ALU.mult), reads=[b_x[ti], b_stat[8], b_condB], writes=[b_ost[ob]])
                P.emit(dve, lambda e, ob=ob: e.tensor_tensor(out=ost[:, ob, :], in0=ost[:, ob, :], in1=sf_bc[:], op=ALU.add),
                       reads=[b_ost[ob], b_condB], writes=[b_ost[ob]])
                P.dma(sp, y_d[ti * 128:(ti + 1) * 128, :], ost[:, ob, :], d_out[ob], reads=[b_ost[ob]])


    except StopBuild:
        pass
    P._wait(sp, [(d.h, d.count) for d in d_out])
    P.barrier()
    for cm in reversed(ctxs):
        cm.__exit__(None, None, None)
    P.close()
    return nc


_NC_CACHE = {}


def _host_layout(inputs):
    f = lambda a: np.ascontiguousarray(np.asarray(a, dtype=np.float32))
    x = f(inputs["x"]); c = f(inputs["c"])
    col = lambda v: f(v).reshape(-1, 128).T
    rowbc = lambda v: np.broadcast_to(f(v).reshape(1, -1), (128, f(v).size))
    ada_b = f(inputs["ada_b"])[0]
    rows = np.empty((128, NROW), np.float32)
    rows[:, R_LNG:R_LNG + 512] = rowbc(inputs["a_ln_g"][0])
    rows[:, R_LNB:R_LNB + 512] = rowbc(inputs["a_ln_b"][0])
    bs = f(inputs["a_spatial_b"])[0]
    rows[:, R_BS:R_BS + 512] = np.repeat(bs.T[:, :, None], 64, axis=2).reshape(128, 512)
    rows[:, R_GF:R_GF + D] = rowbc(inputs["norm_f_g"])
    rows[:, R_ABG1:R_ABG1 + D] = rowbc(ada_b[2 * D:3 * D])
    rows[:, R_ABG2:R_ABG2 + D] = rowbc(ada_b[5 * D:6 * D])
    af_b = f(inputs["ada_f_b"])
    rows[:, R_ABSF:R_ABSF + D] = rowbc(af_b[0:D])
    rows[:, R_ABSCF:R_ABSCF + D] = rowbc(af_b[D:2 * D])
    b_in = f(inputs["b_in"])[0]
    rows[:, R_BINA:R_BINA + D] = rowbc(b_in[0:D])
    wsp = np.ascontiguousarray(f(inputs["a_spatial_w"])[0].transpose(2, 0, 1))
    shared = {
        "rows": rows, "wsp": wsp,
        "ada_w": f(inputs["ada_w"])[0], "ada_f_w": f(inputs["ada_f_w"]),
        "w_in": f(inputs["w_in"])[0], "w_out": f(inputs["w_out"])[0],
        "w_ffn_in": f(inputs["w_ffn_in"])[0], "w_ffn_out": f(inputs["w_ffn_out"])[0],
    }
    base_cols = np.zeros((128, NCOL), np.float32)
    base_cols[:, C_G1:C_G1 + 8] = col(inputs["norm1_g"][0])
    base_cols[:, C_G2:C_G2 + 8] = col(inputs["norm2_g"][0])
    base_cols[:, C_BB:C_BB + 8] = col(b_in[D:2 * D])
    base_cols[:, C_CB:C_CB + 4] = col(inputs["b_conv_b"][0])
    base_cols[:, C_GNG:C_GNG + 4] = col(inputs["b_gn_g"][0])
    base_cols[:, C_GNB:C_GNB + 4] = col(inputs["b_gn_b"][0])
    base_cols[:, C_GA:C_GA + 4] = col(inputs["out_norm_a_g"][0])
    base_cols[:, C_GB:C_GB + 4] = col(inputs["out_norm_b_g"][0])
    base_cols[:, C_AB + 0:C_AB + 8] = col(ada_b[0:D])
    base_cols[:, C_AB + 8:C_AB + 16] = col(ada_b[D:2 * D])
    base_cols[:, C_AB + 16:C_AB + 24] = col(ada_b[3 * D:4 * D])
    base_cols[:, C_AB + 24:C_AB + 32] = col(ada_b[4 * D:5 * D])
    cw = f(inputs["b_conv_w"])[0]
    for cc in range(4):
        base_cols[:, C_CW + cc * 31:C_CW + (cc + 1) * 31] = cw[:, cc * 128:(cc + 1) * 128].T
    in_maps = []
    for core in range(NCORES):
        b, q = core // 4, core % 4
        cols = base_cols.copy()
        cols[:, C_CC:C_CC + 8] = col(c[b])
        cols[:, C_HM] = 0.0 if q == 0 else 1.0
        xs = x[b, q * TOK:(q + 1) * TOK, :]
        if q == 0:
            xh = np.zeros((HALO, D), np.float32)
        else:
            xh = x[b, q * TOK - HALO:q * TOK, :]
        m = {"x": np.ascontiguousarray(xs), "xh": np.ascontiguousarray(xh), "cols": cols}
        m.update(shared)
        in_maps.append(m)
    return in_maps


def kernel(**inputs):
    if "nc" not in _NC_CACHE:
        _NC_CACHE["nc"] = build_nc()
    nc = _NC_CACHE["nc"]
    in_maps = _host_layout(inputs)
    res = run_bass_kernel_spmd(nc, in_maps, core_ids=list(range(NCORES)))
    out = np.empty((2, 4 * TOK, D), np.float32)
    for core in range(NCORES):
        b, q = core // 4, core % 4
        out[b, q * TOK:(q + 1) * TOK, :] = np.asarray(res.results[core]["y"], dtype=np.float32)
    return out
```

```python
import numpy as np
import concourse.bass as bass
import concourse.mybir as mybir
from concourse.bass_utils import run_bass_kernel_spmd

F32 = mybir.dt.float32
BF16 = mybir.dt.bfloat16
AF = mybir.ActivationFunctionType
ALU = mybir.AluOpType
AX = mybir.AxisListType

D = 1024
TOK = 2048
ST = 512
NST = TOK // ST
HALO = 32
DFF = 2816
NJP = 11
EPS = 1e-6
NCORES = 8

C_G1, C_G2, C_BB, C_CB, C_GNG, C_GNB, C_GA, C_GB, C_CC, C_HM = 0, 8, 16, 24, 28, 32, 36, 40, 44, 52
C_AB = 53
C_CW = 85
NCOL = C_CW + 124
R_LNG, R_LNB, R_BS, R_GF, R_ABG1, R_ABG2, R_ABSF, R_ABSCF, R_BINA = 0, 512, 1024, 1536, 2560, 3584, 4608, 5632, 6656
NROW = 7680


import os
KSTOP = int(os.environ.get("KSTOP", "99"))


class StopBuild(Exception):
    pass


def stop_at(n):
    if KSTOP == n:
        raise StopBuild()


class Buf:
    __slots__ = ("w", "r", "name")

    def __init__(self, name=""):
        self.w = None
        self.r = []
        self.name = name


class Eng:
    def __init__(self, name, h, sem, is_pe=False):
        self.name, self.h, self.sem, self.is_pe = name, h, sem, is_pe
        self.count = 0
        self.seen = {}


class DSem:
    def __init__(self, h):
        self.h = h
        self.count = 0


class Prog:
    def __init__(self, nc):
        self.nc = nc
        self._sems = []
        mk = lambda n: self._sem(n)
        self.pe = Eng("pe", nc.tensor, mk("s_pe"), True)
        self.act = Eng("act", nc.scalar, mk("s_act"))
        self.dve = Eng("dve", nc.vector, mk("s_dve"))
        self.pool = Eng("pool", nc.gpsimd, mk("s_pool"))
        self.sp = Eng("sp", nc.sync, mk("s_sp"))
        self.engs = [self.pe, self.act, self.dve, self.pool, self.sp]
        self.dsems = []

    def _sem(self, name):
        cm = self.nc.semaphore(name)
        h = cm.__enter__()
        self._sems.append(cm)
        return h

    def dsem(self, name):
        d = DSem(self._sem(name))
        self.dsems.append(d)
        return d

    def _wait(self, eng, deps):
        need = {}
        for (sem, val) in deps:
            if eng.is_pe and sem is eng.sem:
                continue
            if eng.seen.get(id(sem), 0) >= val:
                continue
            if need.get(id(sem), (None, 0))[1] < val:
                need[id(sem)] = (sem, val)
        for k, (sem, val) in need.items():
            eng.h.wait_ge(sem, val)
            eng.seen[k] = val

    def _deps(self, reads, writes):
        deps = []
        for b in reads:
            if b.w is not None:
                deps.append(b.w)
        for b in writes:
            if b.w is not None:
                deps.append(b.w)
            deps.extend(b.r)
        return deps

    def emit(self, eng, fn, reads=(), writes=(), inc=True):
        self._wait(eng, self._deps(reads, writes))
        ins = fn(eng.h)
        if inc:
            eng.count += 1
            ins.then_inc(eng.sem, 1)
            tok = (eng.sem, eng.count)
        else:
            tok = (eng.sem, eng.count + 1)
        for b in reads:
            b.r.append(tok)
        for b in writes:
            b.w = tok
            b.r = []
        return tok

    def dma(self, q, out_ap, in_ap, dsem, reads=(), writes=()):
        self._wait(q, self._deps(reads, writes))
        ins = q.h.dma_start(out=out_ap, in_=in_ap)
        dsem.count += 16
        ins.then_inc(dsem.h, 16)
        tok = (dsem.h, dsem.count)
        for b in reads:
            b.r.append(tok)
        for b in writes:
            b.w = tok
            b.r = []
        return tok

    def barrier(self):
        toks = [(e.sem, e.count) for e in self.engs if e.count > 0]
        toks += [(d.h, d.count) for d in self.dsems if d.count > 0]
        for e in self.engs:
            self._wait(e, toks)

    def close(self):
        for cm in reversed(self._sems):
            cm.__exit__(None, None, None)


def build_nc():
    nc = bass.Bass("TRN2", target_bir_lowering=False)
    dr = lambda n, shp, kind="ExternalInput": nc.dram_tensor(n, shp, F32, kind=kind).ap()
    x_d = dr("x", [TOK, D])
    xh_d = dr("xh", [HALO, D])
    cols_d = dr("cols", [128, NCOL])
    rows_d = dr("rows", [128, NROW])
    wsp_d = dr("wsp", [128, 8, 128])
    adaw_d = dr("ada_w", [D, 6 * D])
    adafw_d = dr("ada_f_w", [D, 2 * D])
    win_d = dr("w_in", [D, 2 * D])
    wout_d = dr("w_out", [D, D])
    wfi_d = dr("w_ffn_in", [D, 2 * DFF])
    wfo_d = dr("w_ffn_out", [DFF, D])
    y_d = dr("y", [TOK, D], kind="ExternalOutput")
    scr_d = dr("cond_scr", [128, 3 * D], kind="Internal")
    scr_wi = nc.dram_tensor("scr_wi", [6, 128, 8, 2, 512], BF16, kind="Internal").ap()
    scr_wo = nc.dram_tensor("scr_wo", [128, 22, D], BF16, kind="Internal").ap()

    P = Prog(nc)
    pe, act, dve, pool, sp = P.pe, P.act, P.dve, P.pool, P.sp
    ctxs = []

    def sb(name, shape, dt=F32):
        cm = nc.sbuf_tensor("s_" + name, shape, dt)
        t = cm.__enter__()
        ctxs.append(cm)
        return t

    banks = []
    for i in range(8):
        cm = nc.psum_tensor(f"bank{i}", [128, 512], F32)
        banks.append(cm.__enter__())
        ctxs.append(cm)
    bank_bufs = [Buf(f"bank{i}") for i in range(8)]
    rr = [0]

    rr_n = [5]

    def palloc():
        i = rr[0] % rr_n[0]
        rr[0] += 1
        return banks[i], bank_bufs[i]

    cols = sb("cols", [128, NCOL]); b_cols = Buf("cols")
    identf = sb("identf", [128, 128]); b_identf = Buf()
    identb = sb("identb", [128, 128], BF16); b_identb = Buf()
    jb = sb("jb", [128, 128], BF16); b_jb = Buf()
    onesm = sb("onesm", [128, 128], BF16); b_onesm = Buf()
    ones1 = sb("ones1", [1, 128], BF16); b_ones1 = Buf()
    ccols = sb("ccols", [128, 36]); b_ccols = Buf()
    CS1, CA1, CS2, CA2, CCBP = 0, 8, 16, 24, 32
    x_sb = sb("x_sb", [128, 16, D]); b_x = [Buf(f"x{i}") for i in range(16)]
    stat = sb("stat", [128, 64]); b_stat = [Buf(f"stat{i}") for i in range(16)]

    d_cols = P.dsem("d_cols"); d_rowc = P.dsem("d_rowc"); d_stage = P.dsem("d_stage"); d_xh = P.dsem("d_xh")
    d_wspf = P.dsem("d_wspf"); d_wstage = P.dsem("d_wstage"); d_ostage = P.dsem("d_ostage")
    d_x = [P.dsem(f"d_x{i}") for i in range(NST)]
    d_w = P.dsem("d_w")
    d_ring = [P.dsem(f"d_ring{i}") for i in range(4)]
    d_out = [P.dsem(f"d_out{i}") for i in range(2)]
    d_cast = P.dsem("d_cast")

    try:
        P.dma(sp, cols[:], cols_d[:, :], d_cols, writes=[b_cols])

        P.emit(pool, lambda e: e.memset(identf[:], 0.0), writes=[b_identf])
        P.emit(pool, lambda e: e.affine_select(out=identf[:], in_=identf[:], pattern=[[-1, 128]],
                                               compare_op=ALU.not_equal, fill=1.0, base=0, channel_multiplier=1),
               reads=[b_identf], writes=[b_identf])
        P.emit(dve, lambda e: e.tensor_copy(out=identb[:], in_=identf[:]), reads=[b_identf], writes=[b_identb])
        P.emit(pool, lambda e: e.memset(jb[:], 0.0), writes=[b_jb])
        P.emit(pool, lambda e: e.memset(jb[0:64, 0:64], 1.0 / 64), writes=[b_jb])
        P.emit(pool, lambda e: e.memset(jb[64:128, 64:128], 1.0 / 64), writes=[b_jb])
        P.emit(pool, lambda e: e.memset(onesm[:], 1.0 / 512), writes=[b_onesm])
        P.emit(pool, lambda e: e.memset(ones1[:], 1.0), writes=[b_ones1])

        w_in_sb = sb("w_in_sb", [128, 8, 2 * D], BF16); b_win = Buf()
        w_out_sb = sb("w_out_sb", [128, 8, D], BF16); b_wout = [Buf() for _ in range(8)]
        convL = sb("convL", [128, 4, 31, 128], BF16); b_convL = Buf()
        wct = sb("wct", [128, 8, 128], BF16); b_wct = Buf()
        lng_bc = sb("lng_bc", [128, 512]); lnb_bc = sb("lnb_bc", [128, 512]); bs_full = sb("bs_full", [128, 512])
        b_rowc = Buf()
        binhi = sb("binhi", [1, D], BF16); b_bin = Buf()
        caH = sb("caH", [128, 8, 128], BF16); b_caH = Buf()
        tmpR = sb("tmpR", [128, 256]); b_tmpR = Buf()
        stage = sb("stage", [128, 512]); b_stage = Buf()
        xn = sb("xn", [128, 2, D], BF16); b_xn = [Buf(), Buf()]
        sq2 = sb("sq2", [128, 2, 512], BF16)
        dsq = sq2[:, 0, :]; ybsq = sq2[:, 1, :]; junk = sq2[:].rearrange("p a b -> p (a b)")
        junkx = sb("junkx", [128, D], BF16); b_junkx = Buf()
        b_dsq = Buf(); b_ybsq = Buf()
        actT = sb("actT", [128, 8, ST], BF16); b_actT = [Buf(f"actT{k}") for k in range(8)]
        gu4 = sb("gu4", [128, 4, 512], BF16); b_gu = [Buf() for _ in range(4)]
        gv4 = sb("gv4", [128, 4, 512], BF16); b_gv = [Buf() for _ in range(4)]
        vln = sb("vln", [128, 512], BF16); b_vln = Buf()
        ya = sb("ya", [128, 512]); b_ya = Buf()
        yan2 = sb("yan2", [128, 2, 512], BF16); b_yan2 = [Buf(), Buf()]
        ybuf = sb("ybuf", [128, 4, HALO + ST], BF16); b_ybuf = [Buf() for _ in range(4)]
        sig = sb("sig", [128, 512]); b_sig = Buf()
        vtmp = sig; b_vtmp = b_sig
        lnv = sb("lnv", [128, 512]); b_lnv = Buf()
        dsq2 = sb("dsq2", [128, 2, 512], BF16); b_dsq2 = [Buf(), Buf()]
        cnb = sb("cnb", [128, 512]); b_cnb = Buf()

        P.dma(sp, lng_bc[:], rows_d[:, R_LNG:R_LNG + 512], d_rowc, writes=[b_rowc])
        P.dma(sp, lnb_bc[:], rows_d[:, R_LNB:R_LNB + 512], d_rowc, writes=[b_rowc])
        P.dma(sp, bs_full[:], rows_d[:, R_BS:R_BS + 512], d_rowc, writes=[b_rowc])
        P.dma(sp, stage[0:1, 0:256], rows_d[0:1, R_BINA:R_BINA + 256], d_stage, writes=[b_stage])
        for q in range(4):
            if q > 0:
                P.dma(sp, stage[0:1, 0:256], rows_d[0:1, R_BINA + 256 * q:R_BINA + 256 * (q + 1)], d_stage, writes=[b_stage])
            P.emit(dve, lambda e, q=q: e.tensor_copy(out=binhi[0:1, 256 * q:256 * (q + 1)], in_=stage[0:1, 0:256]),
                   reads=[b_stage], writes=[b_bin])
        P.dma(sp, x_sb[:, 0:4, :], x_d[0:ST, :].rearrange("(i p) d -> p i d", p=128), d_x[0], writes=b_x[0:4])
        xh_t = x_sb[0:HALO, 4, :]; b_xh = b_x[4]
        P.dma(sp, xh_t, xh_d[:, :], d_xh, writes=[b_xh])
        wspf = x_sb[:, 5, :].rearrange("p (h t) -> p h t", h=8); b_wspf = b_x[5]
        P.dma(sp, wspf, wsp_d[:, :, :], d_wspf, writes=[b_wspf])
        P.emit(pool, lambda e: e.affine_select(out=wspf, in_=wspf, pattern=[[0, 8], [1, 128]],
                                               compare_op=ALU.is_ge, fill=0.0, base=0, channel_multiplier=-1),
               reads=[b_wspf], writes=[b_wspf])
        P.emit(dve, lambda e: e.tensor_copy(out=wct[:], in_=wspf), reads=[b_wspf], writes=[b_wct])
        bmat = sb("bmat", [128, 128]); b_bmat = Buf()
        P.emit(dve, lambda e: e.tensor_tensor(out=bmat[:], in0=identf[:], in1=jb[:], op=ALU.subtract),
               reads=[b_identf, b_jb], writes=[b_bmat])
        b_convLc = [Buf() for _ in range(4)]

        def gen_convL():
            for cc in range(4):
                for k in range(31):
                    col = cols[:, C_CW + cc * 31 + k:C_CW + cc * 31 + k + 1]
                    if cc % 2 == 0:
                        P.emit(dve, lambda e, cc=cc, k=k, col=col: e.tensor_scalar(
                            out=convL[:, cc, k, :], in0=bmat[:], scalar1=col, scalar2=None, op0=ALU.mult),
                            reads=[b_bmat, b_cols], writes=[b_convLc[cc]])
                    else:
                        P.emit(act, lambda e, cc=cc, k=k, col=col: e.activation(
                            out=convL[:, cc, k, :], in_=bmat[:], func=AF.Identity, scale=col),
                            reads=[b_bmat, b_cols], writes=[b_convLc[cc]])

        cbb = sb("cbb", [128, 4], BF16); b_cbb = Buf()
        P.emit(dve, lambda e: e.tensor_copy(out=cbb[:], in_=cols[:, C_CB:C_CB + 4]), reads=[b_cols], writes=[b_cbb])
        bk, bb = palloc()
        P.emit(pe, lambda e: e.matmul(bk[:, 0:4], lhsT=jb[:], rhs=cbb[:], start=True, stop=True),
               reads=[b_jb, b_cbb], writes=[bb])
        P.emit(dve, lambda e: e.tensor_tensor(out=ccols[:, CCBP:CCBP + 4], in0=cols[:, C_CB:C_CB + 4], in1=bk[:, 0:4],
                                              op=ALU.subtract), reads=[b_cols, bb], writes=[b_ccols])
        caf = sb("caf", [128, 8]); b_caf = Buf()
        cab = sb("cab", [128, 8], BF16); b_cab = Buf()
        P.emit(act, lambda e: e.activation(out=caf[:], in_=cols[:, C_CC:C_CC + 8], func=AF.Silu),
               reads=[b_cols], writes=[b_caf])
        P.emit(dve, lambda e: e.tensor_copy(out=cab[:], in_=caf[:]), reads=[b_caf], writes=[b_cab])
        for k in range(8):
            P.emit(dve, lambda e, k=k: e.tensor_copy(out=caH[:, k, :], in_=cab[:, k:k + 1].to_broadcast([128, 128])),
                   reads=[b_cab], writes=[b_caH])

        gate1_bc = x_sb[:, 6, :]; b_g1bc = b_x[6]
        wstage = x_sb[:, 7, :]; b_wstage = b_x[7]
        ostage = sb("ostage", [128, 512]); b_ostage = Buf()

        ring_views = []
        ring_bufs = []
        for j in range(4):
            v = x_sb[:, 8 + 2 * j:10 + 2 * j, :].rearrange("p a d -> p (a d)").bitcast(BF16)
            ring_views.append(v.rearrange("p (k n) -> p k n", k=8))
            ring_bufs.append([b_x[8 + 2 * j], b_x[9 + 2 * j]])
        ada_dma_next = [0]

        def ada_slot(b):
            return b % 4 if b < 12 else 2 + (b % 2)

        def adaln_dma(upto):
            while ada_dma_next[0] <= min(upto, 15):
                b = ada_dma_next[0]
                if b < 12:
                    src = adaw_d[:, b * 512:(b + 1) * 512]
                else:
                    src = adafw_d[:, (b - 12) * 512:(b - 11) * 512]
                sl_ = ada_slot(b)
                P.dma(pool, ring_views[sl_], src.rearrange("(k p) n -> p k n", p=128), d_ring[sl_],
                      writes=ring_bufs[sl_])
                ada_dma_next[0] += 1

        def adaln_block(b):
            kind = b // 2
            q = b % 2
            adaln_dma(b)
            ring = ring_views[ada_slot(b)]
            rb = ring_bufs[ada_slot(b)]
            bk, bb = palloc()
            for k in range(8):
                P.emit(pe, lambda e, k=k: e.matmul(bk[:], lhsT=caH[:, k, :], rhs=ring[:, k, :],
                                                   start=(k == 0), stop=(k == 7)),
                       reads=[b_caH] + rb, writes=[bb], inc=(k == 7))
            adaln_dma(min(b + 3, 11) if b + 1 < 12 else b + 1)
            if kind in (0, 1, 3, 4):
                for h in range(4):
                    ch = q * 4 + h
                    P.emit(dve, lambda e, h=h: e.tensor_tensor(out=tmpR[:, 0:128], in0=bk[:, h * 128:(h + 1) * 128],
                                                              in1=identf[:], op=ALU.mult),
                           reads=[bb, b_identf], writes=[b_tmpR])
                    P.emit(dve, lambda e: e.tensor_reduce(out=tmpR[:, 128:129], in_=tmpR[:, 0:128], axis=AX.X, op=ALU.add),
                           reads=[b_tmpR], writes=[b_tmpR])
                    abc = C_AB + {0: 0, 1: 8, 3: 16, 4: 24}[kind] + ch
                    if kind in (0, 3):
                        dst = (CS1 if kind == 0 else CS2) + ch
                        P.emit(dve, lambda e, dst=dst, abc=abc: e.tensor_tensor(
                            out=ccols[:, dst:dst + 1], in0=tmpR[:, 128:129], in1=cols[:, abc:abc + 1], op=ALU.add),
                            reads=[b_tmpR, b_cols], writes=[b_ccols])
                    else:
                        dst = (CA1 if kind == 1 else CA2) + ch
                        gcol = (C_G1 if kind == 1 else C_G2) + ch
                        P.emit(dve, lambda e, abc=abc: e.tensor_scalar(
                            out=tmpR[:, 129:130], in0=tmpR[:, 128:129], scalar1=cols[:, abc:abc + 1], scalar2=1.0,
                            op0=ALU.add, op1=ALU.add), reads=[b_tmpR, b_cols], writes=[b_tmpR])
                        P.emit(dve, lambda e, dst=dst, gcol=gcol: e.tensor_tensor(
                            out=ccols[:, dst:dst + 1], in0=tmpR[:, 129:130], in1=cols[:, gcol:gcol + 1], op=ALU.mult),
                            reads=[b_tmpR, b_cols], writes=[b_ccols])
            else:
                roff = {2: R_ABG1, 5: R_ABG2, 6: R_ABSF, 7: R_ABSCF}[kind] + q * 512
                P.dma(sp, stage[:], rows_d[:, roff:roff + 512], d_stage, writes=[b_stage])
                if kind == 2:
                    P.emit(dve, lambda e: e.tensor_tensor(out=gate1_bc[:, q * 512:(q + 1) * 512], in0=bk[:],
                                                          in1=stage[:], op=ALU.add),
                           reads=[bb, b_stage], writes=[b_g1bc])
                else:
                    P.emit(dve, lambda e: e.tensor_tensor(out=ostage[:], in0=bk[:], in1=stage[:], op=ALU.add),
                           reads=[bb, b_stage], writes=[b_ostage])
                    if kind == 7:
                        P.dma(sp, stage[:], rows_d[:, R_GF + q * 512:R_GF + (q + 1) * 512], d_stage, writes=[b_stage])
                        P.emit(dve, lambda e: e.scalar_tensor_tensor(out=ostage[:], in0=ostage[:], scalar=1.0, in1=stage[:],
                                                                     op0=ALU.add, op1=ALU.mult),
                               reads=[b_ostage, b_stage], writes=[b_ostage])
                    so = {5: 0, 6: D, 7: 2 * D}[kind] + q * 512
                    P.dma(sp, scr_d[:, so:so + 512], ostage[:], d_ostage, reads=[b_ostage])

        adaln_dma(3)
        d_w2 = P.dsem("d_w2")
        b_winA = Buf()
        P.dma(pool, w_in_sb[:, :, D:2 * D], win_d[:, D:2 * D].rearrange("(k p) n -> p k n", p=128), d_w, writes=[b_win])
        P.dma(pool, w_in_sb[:, :, 0:D], win_d[:, 0:D].rearrange("(k p) n -> p k n", p=128), d_w2, writes=[b_winA])
        stop_at(1)
        for b in range(4):
            adaln_block(b)
        stop_at(2)

        def wout_prep():
            for k in range(8):
                P.dma(sp, wstage, wout_d[k * 128:(k + 1) * 128, :], d_wstage, writes=[b_wstage])
                P.emit(pool, lambda e, k=k: e.tensor_tensor(out=wstage, in0=wstage, in1=gate1_bc, op=ALU.mult),
                       reads=[b_wstage, b_g1bc], writes=[b_wstage])
                gcol = (C_GA + k) if k < 4 else (C_GB + k - 4)
                P.emit(pool, lambda e, k=k, gcol=gcol: e.tensor_tensor(out=w_out_sb[:, k, :], in0=wstage,
                                                                      in1=cols[:, gcol:gcol + 1].to_broadcast([128, D]),
                                                                      op=ALU.mult),
                       reads=[b_wstage, b_cols], writes=[b_wout[k]])

        def rstd_from(sum_ap, out_ap, n, dim, rbufs, wbufs):
            P.emit(act, lambda e: e.activation(out=out_ap, in_=sum_ap, func=AF.Ln, bias=EPS, scale=1.0 / dim),
                   reads=rbufs, writes=wbufs)
            P.emit(act, lambda e: e.activation(out=out_ap, in_=out_ap, func=AF.Exp, scale=-0.5),
                   reads=wbufs, writes=wbufs)

        def norm_transpose(src_tiles, src_bufs, ntok, acol, scol, sbuf_stat, dst, dst_bufs, tw, phase=0):
            nt = len(src_tiles)
            ss = stat[0:tw, 0:nt]
            rs = stat[0:tw, 8:8 + nt]
            def do_xn(i):
                t, tb = src_tiles[i], src_bufs[i]
                xb = i % 2
                P.emit(dve, lambda e: e.tensor_scalar(out=xn[0:tw, xb, :], in0=t,
                                                      scalar1=stat[0:tw, 8 + i:9 + i], scalar2=None,
                                                      op0=ALU.mult),
                       reads=[tb, sbuf_stat], writes=[b_xn[xb]])

            if phase in (0, 1):
                for i, (t, tb) in enumerate(zip(src_tiles, src_bufs)):
                    P.emit(act, lambda e, t=t, i=i: e.activation(out=junkx[0:tw, :], in_=t, func=AF.Square,
                                                                 accum_out=stat[0:tw, i:i + 1]),
                           reads=[tb], writes=[b_junkx, sbuf_stat])
                rstd_from(ss, rs, nt, D, [sbuf_stat], [sbuf_stat])
                for i in range(min(2, nt)):
                    do_xn(i)
            if phase == 1:
                return
            pbs = [palloc() for _ in range(4)]
            for i, (t, tb) in enumerate(zip(src_tiles, src_bufs)):
                xb = i % 2
                if i >= 2:
                    do_xn(i)
                for k in range(8):
                    bk, bb = pbs[k // 2]
                    pv = bk[:].bitcast(BF16)
                    c0 = (k % 2) * 512 + i * tw
                    P.emit(pe, lambda e, pv=pv, c0=c0, k=k, xb=xb: e.transpose(
                        out=pv[:, c0:c0 + tw], in_=xn[0:tw, xb, k * 128:(k + 1) * 128], identity=identb[0:tw, 0:tw]),
                        reads=[b_xn[xb], b_identb], writes=[bb], inc=(k == 7))
            if tw == 128: stop_at(32)
            n = nt * tw
            for k in range(8):
                bk, bb = pbs[k // 2]
                pv = bk[:].bitcast(BF16)
                c0 = (k % 2) * 512
                eng = act if (k // 2) % 2 == 0 else dve
                if os.environ.get('KEV') == 'act': eng = act
                if os.environ.get('KEV') == 'dve': eng = dve
                if eng is act:
                    P.emit(act, lambda e, pv=pv, c0=c0, k=k: e.activation(
                        out=dst[:, k, 0:n], in_=pv[:, c0:c0 + n], func=AF.Identity,
                        bias=ccols[:, scol + k:scol + k + 1], scale=ccols[:, acol + k:acol + k + 1]),
                        reads=[bb, b_ccols], writes=[dst_bufs[k]])
                else:
                    P.emit(dve, lambda e, pv=pv, c0=c0, k=k: e.tensor_scalar(
                        out=dst[:, k, 0:n], in0=pv[:, c0:c0 + n], scalar1=ccols[:, acol + k:acol + k + 1],
                        scalar2=ccols[:, scol + k:scol + k + 1], op0=ALU.mult, op1=ALU.add),
                        reads=[bb, b_ccols], writes=[dst_bufs[k]])

        def b_branch_y(n, col0, mask):
            for cc in range(4):
                bv, bbv = palloc()
                bg, bbg = palloc()
                for (bk, bb, coff) in ((bv, bbv, D + cc * 128), (bg, bbg, D + 512 + cc * 128)):
                    for k in range(8):
                        P.emit(pe, lambda e, bk=bk, k=k, coff=coff: e.matmul(
                            bk[:, 0:n], lhsT=w_in_sb[:, k, coff:coff + 128], rhs=actT[:, k, 0:n],
                            start=(k == 0), stop=(k == 7)),
                            reads=[b_win, b_actT[k]], writes=[bb], inc=(k == 7))
                P.emit(act, lambda e, bg=bg, cc=cc: e.activation(out=sig[:, 0:n], in_=bg[:, 0:n], func=AF.Sigmoid,
                                                                bias=cols[:, C_BB + 4 + cc:C_BB + 5 + cc]),
                       reads=[bbg, b_cols], writes=[b_sig])
                if mask:
                    P.emit(dve, lambda e: e.tensor_scalar(out=sig[:, 0:n], in0=sig[:, 0:n], scalar1=cols[:, C_HM:C_HM + 1],
                                                          scalar2=None, op0=ALU.mult),
                           reads=[b_sig, b_cols], writes=[b_sig])
                P.emit(dve, lambda e, bv=bv, cc=cc: e.scalar_tensor_tensor(
                    out=ybuf[:, cc, col0:col0 + n], in0=bv[:, 0:n], scalar=cols[:, C_BB + cc:C_BB + cc + 1],
                    in1=sig[:, 0:n], op0=ALU.add, op1=ALU.mult),
                    reads=[bbv, b_cols, b_sig], writes=[b_ybuf[cc]])

        stop_at(3)
        norm_transpose([xh_t], [b_xh], HALO, CA1, CS1, b_stat[1], actT, b_actT, HALO)
        b_branch_y(HALO, 0, True)

        stop_at(4)
        ada_next = [4]

        def ada_fill(n):
            for _ in range(n):
                if ada_next[0] < 16:
                    adaln_block(ada_next[0])
                    ada_next[0] += 1

        def prefetch_x(sn):
            P.dma(sp, x_sb[:, 4 * sn:4 * (sn + 1), :],
                  x_d[sn * ST:(sn + 1) * ST, :].rearrange("(i p) d -> p i d", p=128), d_x[sn],
                  writes=b_x[4 * sn:4 * (sn + 1)])

        def front_tiles(sn):
            return [x_sb[:, 4 * sn + i, :] for i in range(4)], b_x[4 * sn:4 * sn + 4]

        t0_, tb0_ = front_tiles(0)
        norm_transpose(t0_, tb0_, 128, CA1, CS1, b_stat[1], actT, b_actT, 128, phase=1)
        for s in range(NST):
            if 1 <= s and s + 1 < NST:
                assert ada_next[0] >= (12 if s == 1 else 16)
                prefetch_x(s + 1)
            tiles, tbufs = front_tiles(s)
            norm_transpose(tiles, tbufs, 128, CA1, CS1, b_stat[1], actT, b_actT, 128, phase=2)
            if s == 0:
                gen_convL()
            for i in range(4):
                for half, (dst, dbuf) in enumerate(((gu4, b_gu[i]), (gv4, b_gv[i]))):
                    bk, bb = palloc()
                    for k in range(8):
                        P.emit(pe, lambda e, bk=bk, k=k, i=i, half=half: e.matmul(
                            bk[:], lhsT=actT[:, k, i * 128:(i + 1) * 128], rhs=w_in_sb[:, k, half * 512:(half + 1) * 512],
                            start=(k == 0), stop=False), reads=[b_actT[k], b_winA], writes=[bb], inc=False)
                    P.emit(pe, lambda e, bk=bk, half=half: e.matmul(
                        bk[:], lhsT=ones1[0:1, :], rhs=binhi[0:1, half * 512:(half + 1) * 512], start=False, stop=True),
                        reads=[b_ones1, b_bin], writes=[bb])
                    P.emit(act, lambda e, bk=bk, dst=dst, i=i: e.activation(out=dst[:, i, :], in_=bk[:], func=AF.Gelu),
                           reads=[bb], writes=[dbuf])
                P.emit(dve, lambda e, i=i: e.bn_stats(out=stat[:, 16 + 6 * i:22 + 6 * i], in_=gv4[:, i, :]),
                       reads=[b_gv[i]], writes=[b_stat[2]])
                P.emit(dve, lambda e, i=i: e.bn_aggr(out=stat[:, 40 + 2 * i:42 + 2 * i], in_=stat[:, 16 + 6 * i:22 + 6 * i]),
                       reads=[b_stat[2]], writes=[b_stat[3]])
            if s == 0:
                ada_fill(2)
                wout_prep()
                prefetch_x(1)
                ada_fill(1)
            if s == 1:
                ada_fill(1)
            b_branch_y(ST, HALO, False)
            mv = stat[:, 40:48].rearrange("p (i t) -> p i t", t=2)
            P.emit(act, lambda e: e.activation(out=stat[:, 48:52], in_=mv[:, :, 1], func=AF.Ln, bias=EPS, scale=1.0),
                   reads=[b_stat[3]], writes=[b_stat[4]])
            P.emit(act, lambda e: e.activation(out=stat[:, 48:52], in_=stat[:, 48:52], func=AF.Exp, scale=-0.5),
                   reads=[b_stat[4]], writes=[b_stat[4]])
            if s == 0:
                ada_fill(2)
            if s == 1:
                ada_fill(1)

            bm, bbm = banks[7], bank_bufs[7]
            dbank = {}
            sbank = {}
            ptA = []

            def conv(cc):
                bi_ = (0, 1, 4)[cc % 3]
                bd, bbd = banks[bi_], bank_bufs[bi_]
                dbank[cc] = (bd, bbd)
                for k in range(31):
                    P.emit(pe, lambda e, k=k: e.matmul(
                        bd[:], lhsT=convL[:, cc, k, :], rhs=ybuf[:, cc, HALO - 30 + k:HALO - 30 + k + ST],
                        start=(k == 0), stop=(k == 30)), reads=[b_convLc[cc], b_ybuf[cc]], writes=[bbd], inc=(k == 30))
                P.emit(act, lambda e: e.activation(out=dsq2[:, cc % 2, :], in_=bd[:], func=AF.Square,
                                                   bias=ccols[:, CCBP + cc:CCBP + cc + 1]),
                       reads=[bbd, b_ccols], writes=[b_dsq2[cc % 2]])
                P.emit(dve, lambda e: e.tensor_copy(out=ybuf[:, cc, 0:HALO], in_=ybuf[:, cc, ST:ST + HALO]),
                       reads=[b_ybuf[cc]], writes=[b_ybuf[cc]])

            def gn_var(cc):
                bd, bbd = dbank[cc]
                bvv, bbvv = banks[2], bank_bufs[2]
                P.emit(pe, lambda e: e.matmul(bvv[:], lhsT=jb[:], rhs=dsq2[:, cc % 2, :], start=True, stop=True),
                       reads=[b_jb, b_dsq2[cc % 2]], writes=[bbvv])
                P.emit(act, lambda e: e.activation(out=lnv[:], in_=bvv[:], func=AF.Ln, bias=EPS, scale=1.0),
                       reads=[bbvv], writes=[b_lnv])
                P.emit(act, lambda e: e.activation(out=lnv[:], in_=lnv[:], func=AF.Exp, scale=-0.5),
                       reads=[b_lnv], writes=[b_lnv])
                P.emit(dve, lambda e: e.scalar_tensor_tensor(out=cnb[:], in0=bd[:], scalar=ccols[:, CCBP + cc:CCBP + cc + 1],
                                                             in1=lnv[:], op0=ALU.add, op1=ALU.mult),
                       reads=[bbd, b_ccols, b_lnv], writes=[b_cnb])
                P.emit(dve, lambda e: e.tensor_scalar(out=cnb[:], in0=cnb[:], scalar1=cols[:, C_GNG + cc:C_GNG + cc + 1],
                                                      scalar2=cols[:, C_GNB + cc:C_GNB + cc + 1], op0=ALU.mult, op1=ALU.add),
                       reads=[b_cnb, b_cols], writes=[b_cnb])
                P.emit(act, lambda e: e.activation(out=lnv[:], in_=cnb[:], func=AF.Exp, scale=-1.0),
                       reads=[b_cnb], writes=[b_lnv])
                P.emit(act, lambda e: e.activation(out=lnv[:], in_=lnv[:], func=AF.Ln, bias=1.0, scale=1.0),
                       reads=[b_lnv], writes=[b_lnv])
                P.emit(act, lambda e: e.activation(out=lnv[:], in_=lnv[:], func=AF.Exp, scale=-1.0),
                       reads=[b_lnv], writes=[b_lnv])
                P.emit(dve, lambda e: e.tensor_tensor(out=actT[:, 4 + cc, :], in0=cnb[:], in1=lnv[:], op=ALU.mult),
                       reads=[b_cnb, b_lnv], writes=[b_actT[4 + cc]])
                P.emit(act, lambda e: e.activation(out=ybsq, in_=actT[:, 4 + cc, :], func=AF.Square),
                       reads=[b_actT[4 + cc]], writes=[b_ybsq])

            def gn_msq(cc):
                for i in range(4):
                    P.emit(pe, lambda e, i=i: e.matmul(bm[:, i * 4 + cc:i * 4 + cc + 1], lhsT=ybsq[:, i * 128:(i + 1) * 128],
                                                       rhs=onesm[:, 0:1], start=True, stop=True),
                           reads=[b_onesm, b_ybsq], writes=[bbm], inc=(i == 3))

            def a_pre(i):
                P.emit(dve, lambda e: e.tensor_scalar(out=vtmp[:], in0=gv4[:, i, :], scalar1=stat[:, 40 + 2 * i:41 + 2 * i],
                                                      scalar2=stat[:, 48 + i:49 + i], op0=ALU.subtract, op1=ALU.mult),
                       reads=[b_gv[i], b_stat[3], b_stat[4]], writes=[b_vtmp])
                P.emit(dve, lambda e: e.tensor_tensor(out=vtmp[:], in0=vtmp[:], in1=lng_bc[:], op=ALU.mult),
                       reads=[b_vtmp, b_rowc], writes=[b_vtmp])
                P.emit(dve, lambda e: e.tensor_tensor(out=vln[:], in0=vtmp[:], in1=lnb_bc[:], op=ALU.add),
                       reads=[b_vtmp, b_rowc], writes=[b_vln])
                bk, bb = banks[3], bank_bufs[3]
                sbank[i] = (bk, bb)
                for h in range(8):
                    P.emit(pe, lambda e, h=h: e.matmul(bk[:, h * 64:(h + 1) * 64], lhsT=wct[:, h, :],
                                                       rhs=vln[:, h * 64:(h + 1) * 64], start=True, stop=True),
                           reads=[b_wct, b_vln], writes=[bb], inc=(h == 7))

            def a_post(i):
                bk, bb = sbank[i]
                yb_ = i % 2
                P.emit(dve, lambda e: e.tensor_tensor(out=ya[:], in0=bk[:], in1=bs_full[:], op=ALU.add),
                       reads=[bb, b_rowc], writes=[b_ya])
                P.emit(dve, lambda e: e.tensor_tensor(out=yan2[:, yb_, :], in0=ya[:], in1=gu4[:, i, :], op=ALU.mult),
                       reads=[b_ya, b_gu[i]], writes=[b_yan2[yb_]])
                P.emit(act, lambda e: e.activation(out=junk[:, 0:512], in_=yan2[:, yb_, :], func=AF.Square,
                                                   accum_out=stat[:, 52 + i:53 + i]),
                       reads=[b_yan2[yb_]], writes=[b_dsq, b_stat[5]])
                if i == 0:
                    ptA.extend([(banks[5], bank_bufs[5]), (banks[6], bank_bufs[6])])
                for c in range(4):
                    bk2, bb2 = ptA[c // 2]
                    pv = bk2[:].bitcast(BF16)
                    c0 = (c % 2) * 512 + i * 128
                    P.emit(pe, lambda e, pv=pv, c0=c0, c=c: e.transpose(out=pv[:, c0:c0 + 128],
                                                                       in_=yan2[:, yb_, c * 128:(c + 1) * 128],
                                                                       identity=identb[:]),
                           reads=[b_yan2[yb_], b_identb], writes=[bb2], inc=(c == 3))

            conv(0)
            a_pre(0)
            conv(1)
            a_post(0)
            a_pre(1)
            gn_var(0)
            conv(2)
            a_post(1)
            a_pre(2)
            gn_msq(0)
            gn_var(1)
            conv(3)
            a_post(2)
            a_pre(3)
            gn_msq(1)
            gn_var(2)
            a_post(3)
            for c in range(4):
                bk2, bb2 = ptA[c // 2]
                pv = bk2[:].bitcast(BF16)
                c0 = (c % 2) * 512
                P.emit(dve, lambda e, pv=pv, c0=c0, c=c: e.tensor_copy(out=actT[:, c, :], in_=pv[:, c0:c0 + 512]),
                       reads=[bb2], writes=[b_actT[c]])
            gn_msq(2)
            gn_var(3)
            if s == 0:
                ada_fill(3)
            if s == 1:
                ada_fill(2)
                assert ada_next[0] >= 16
                for q_ in range(6):
                    c0_, w_ = (q_ * 512, 512) if q_ < 5 else (2560, 256)
                    for g_ in range(2):
                        P.dma(pool, scr_wi[q_, :, :, g_, 0:w_],
                              wfi_d[:, g_ * DFF + c0_:g_ * DFF + c0_ + w_].rearrange("(k p) n -> p k n", p=128), d_cast)
                for h_ in range(2):
                    P.dma(pool, scr_wo[:, 11 * h_:11 * (h_ + 1), :],
                          wfo_d[11 * h_ * 128:11 * (h_ + 1) * 128, :].rearrange("(j p) n -> p j n", p=128), d_cast)
            if s + 1 < NST:
                tn, tbn = front_tiles(s + 1)
                norm_transpose(tn, tbn, 128, CA1, CS1, b_stat[1], actT, b_actT, 128, phase=1)
            rstd_from(stat[:, 52:56], stat[:, 56:60], 4, 512, [b_stat[5]], [b_stat[6]])
            for i in range(4):
                for hf in range(2):
                    pa, bpa = palloc()
                    pb_, bpb = palloc()
                    for (bk, bb, k0) in ((pa, bpa, 0), (pb_, bpb, 4)):
                        for k in range(k0, k0 + 4):
                            P.emit(pe, lambda e, bk=bk, k=k, k0=k0: e.matmul(
                                bk[:], lhsT=actT[:, k, i * 128:(i + 1) * 128], rhs=w_out_sb[:, k, hf * 512:(hf + 1) * 512],
                                start=(k == k0), stop=(k == k0 + 3)), reads=[b_actT[k], b_wout[k]], writes=[bb],
                                inc=(k == k0 + 3))
                    if i == 0 and hf == 0:
                        gn_msq(3)
                        P.emit(dve, lambda e: e.tensor_reduce(out=stat[:, 60:64],
                                                              in_=bm[:, 0:16].rearrange("p (i c) -> p i c", c=4),
                                                              axis=AX.X, op=ALU.add), reads=[bbm], writes=[b_stat[9]])
                        rstd_from(stat[:, 60:64], stat[:, 60:64], 4, 1, [b_stat[9]], [b_stat[9]])
                    xt = x_sb[:, 4 * s + i, hf * 512:(hf + 1) * 512]
                    P.emit(dve, lambda e: e.scalar_tensor_tensor(out=xt, in0=pa[:], scalar=stat[:, 56 + i:57 + i], in1=xt,
                                                                 op0=ALU.mult, op1=ALU.add),
                           reads=[bpa, b_stat[6], b_x[4 * s + i]], writes=[b_x[4 * s + i]])
                    P.emit(dve, lambda e: e.scalar_tensor_tensor(out=xt, in0=pb_[:], scalar=stat[:, 60 + i:61 + i], in1=xt,
                                                                 op0=ALU.mult, op1=ALU.add),
                           reads=[bpb, b_stat[9], b_x[4 * s + i]], writes=[b_x[4 * s + i]])
            stop_at(5 + s)
        assert ada_next[0] >= 16
        stop_at(9)
        rr_n[0] = 8
        P.barrier()
        nA = len(ctxs)
        keep = 8 + 9
        for cm in reversed(ctxs[keep:]):
            cm.__exit__(None, None, None)
        del ctxs[keep:]

        gate2_bc = sb("gate2_bc", [128, D]); af_bc = sb("af_bc", [128, D]); sf_bc = sb("sf_bc", [128, D])
        b_condB = Buf()
        h2T = sb("h2T", [128, 8, ST], BF16); b_h2T = [Buf() for _ in range(8)]
        aT = sb("aT", [128, 22, ST], BF16); b_aT = [Buf() for _ in range(22)]
        wo_sb = sb("wo_sb", [128, 22, D], BF16); b_wo = Buf()
        wi_ring = sb("wi_ring", [128, 2, 8, 2, 512], BF16); b_wi = [Buf() for _ in range(2)]
        xn2 = sb("xn2", [128, 2, D], BF16); b_xn2 = [Buf(), Buf()]
        junk2 = sb("junk2", [128, D], BF16); b_junk2 = Buf()
        sg = sb("sg", [128, 2, 512]); b_sg = [Buf(), Buf()]
        ost = sb("ost", [128, 2, D]); b_ost = [Buf(), Buf()]
        d_cb = P.dsem("d_cb")
        d_wo = P.dsem("d_wo")
        d_wi = [P.dsem(f"d_wi{i}") for i in range(2)]

        P.dma(sp, gate2_bc[:], scr_d[:, 0:D], d_cb, writes=[b_condB])
        P.dma(sp, sf_bc[:], scr_d[:, D:2 * D], d_cb, writes=[b_condB])
        P.dma(sp, af_bc[:], scr_d[:, 2 * D:3 * D], d_cb, writes=[b_condB])

        NLD = 6
        NL = NST * NLD
        issued = [0]
        wfi_v = wfi_d.rearrange("(k p) (g c) -> p k g c", p=128, g=2)
        wo_loaded = [False]

        def load_cols(q):
            return (q * 512, 512) if q < 5 else (2560, 256)

        def wi_ensure(upto):
            while issued[0] <= min(upto, NL - 1):
                g = issued[0]
                q_ = g % NLD
                c0, w = load_cols(q_)
                sl = g % 2
                P.dma(sp, wi_ring[:, sl, :, :, 0:w], scr_wi[q_, :, :, :, 0:w], d_wi[sl], writes=[b_wi[sl]])
                issued[0] += 1
                if g == 1 and not wo_loaded[0]:
                    wo_loaded[0] = True
                    P.dma(sp, wo_sb[:], scr_wo[:, :, :], d_wo, writes=[b_wo])

        wi_ensure(1)

        def norm_transpose_b(s, phase=0):
            ss = stat[:, 0:4]
            rs = stat[:, 8:12]

            def do_xn2(i):
                xb = i % 2
                P.emit(dve, lambda e: e.tensor_scalar(out=xn2[:, xb, :], in0=x_sb[:, 4 * s + i, :],
                                                      scalar1=stat[:, 8 + i:9 + i], scalar2=None, op0=ALU.mult),
                       reads=[b_x[4 * s + i], b_stat[1]], writes=[b_xn2[xb]])

            for i in (range(4) if phase in (0, 1) else ()):
                P.emit(act, lambda e, i=i: e.activation(out=junk2[:], in_=x_sb[:, 4 * s + i, :], func=AF.Square,
                                                        accum_out=stat[:, i:i + 1]),
                       reads=[b_x[4 * s + i]], writes=[b_junk2, b_stat[1]])
            if phase in (0, 1):
                P.emit(act, lambda e: e.activation(out=rs, in_=ss, func=AF.Ln, bias=EPS, scale=1.0 / D),
                       reads=[b_stat[1]], writes=[b_stat[1]])
                P.emit(act, lambda e: e.activation(out=rs, in_=rs, func=AF.Exp, scale=-0.5),
                       reads=[b_stat[1]], writes=[b_stat[1]])
                do_xn2(0)
                do_xn2(1)
            if phase == 1:
                return
            pbs = [palloc() for _ in range(4)]
            for i in range(4):
                xb = i % 2
                if i >= 2:
                    do_xn2(i)
                for k in range(8):
                    bk, bb = pbs[k // 2]
                    pv = bk[:].bitcast(BF16)
                    c0 = (k % 2) * 512 + i * 128
                    P.emit(pe, lambda e, pv=pv, c0=c0, k=k, xb=xb: e.transpose(
                        out=pv[:, c0:c0 + 128], in_=xn2[:, xb, k * 128:(k + 1) * 128], identity=identb[:]),
                        reads=[b_xn2[xb], b_identb], writes=[bb], inc=(k == 7))
            for k in range(8):
                bk, bb = pbs[k // 2]
                pv = bk[:].bitcast(BF16)
                c0 = (k % 2) * 512
                if (k // 2) % 2 == 0:
                    P.emit(act, lambda e, pv=pv, c0=c0, k=k: e.activation(
                        out=h2T[:, k, :], in_=pv[:, c0:c0 + 512], func=AF.Identity,
                        bias=ccols[:, CS2 + k:CS2 + k + 1], scale=ccols[:, CA2 + k:CA2 + k + 1]),
                        reads=[bb, b_ccols], writes=[b_h2T[k]])
                else:
                    P.emit(dve, lambda e, pv=pv, c0=c0, k=k: e.tensor_scalar(
                        out=h2T[:, k, :], in0=pv[:, c0:c0 + 512], scalar1=ccols[:, CA2 + k:CA2 + k + 1],
                        scalar2=ccols[:, CS2 + k:CS2 + k + 1], op0=ALU.mult, op1=ALU.add),
                        reads=[bb, b_ccols], writes=[b_h2T[k]])

        oc = [0]
        norm_transpose_b(0)
        for s in range(NST):
            for q in range(NLD):
                g = s * NLD + q
                wi_ensure(g + 1)
                if q == 2 and s + 1 < NST:
                    norm_transpose_b(s + 1, phase=1)
                sl = g % 2
                c0, w = load_cols(q)
                for jj in range(w // 128):
                    j = c0 // 128 + jj
                    bg, bbg = palloc()
                    bu, bbu = palloc()
                    for (bk, bb, g_) in ((bg, bbg, 0), (bu, bbu, 1)):
                        for k in range(8):
                            P.emit(pe, lambda e, bk=bk, k=k, g_=g_, sl=sl, jj=jj: e.matmul(
                                bk[:], lhsT=wi_ring[:, sl, k, g_, jj * 128:(jj + 1) * 128], rhs=h2T[:, k, :],
                                start=(k == 0), stop=(k == 7)), reads=[b_wi[sl], b_h2T[k]], writes=[bb], inc=(k == 7))
                    sb_ = j % 2
                    P.emit(act, lambda e, bg=bg, sb_=sb_: e.activation(out=sg[:, sb_, :], in_=bg[:], func=AF.Silu),
                           reads=[bbg], writes=[b_sg[sb_]])
                    P.emit(dve, lambda e, bu=bu, sb_=sb_, j=j: e.tensor_tensor(out=aT[:, j, :], in0=bu[:], in1=sg[:, sb_, :],
                                                                             op=ALU.mult),
                           reads=[bbu, b_sg[sb_]], writes=[b_aT[j]])
            if s + 1 < NST:
                norm_transpose_b(s + 1, phase=2)
            for i in range(4):
                ti = 4 * s + i
                pbk = []
                for hf in range(2):
                    bk, bb = palloc()
                    pbk.append((bk, bb))
                    for j in range(22):
                        P.emit(pe, lambda e, bk=bk, j=j, i=i, hf=hf: e.matmul(
                            bk[:], lhsT=aT[:, j, i * 128:(i + 1) * 128], rhs=wo_sb[:, j, hf * 512:(hf + 1) * 512],
                            start=(j == 0), stop=(j == 21)), reads=[b_aT[j], b_wo], writes=[bb], inc=(j == 21))
                ob = oc[0] % 2
                oc[0] += 1
                for hf in range(2):
                    bk, bb = pbk[hf]
                    sl_ = slice(hf * 512, (hf + 1) * 512)
                    P.emit(dve, lambda e, bk=bk, sl_=sl_, ob=ob: e.tensor_tensor(out=ost[:, ob, sl_], in0=bk[:],
                                                                                in1=gate2_bc[:, sl_], op=ALU.mult),
                           reads=[bb, b_condB], writes=[b_ost[ob]])
                P.emit(dve, lambda e, ti=ti, ob=ob: e.tensor_tensor(out=x_sb[:, ti, :], in0=x_sb[:, ti, :], in1=ost[:, ob, :],
                                                                   op=ALU.add),
                       reads=[b_x[ti], b_ost[ob]], writes=[b_x[ti]])
                P.emit(act, lambda e, ti=ti, i=i: e.activation(out=junk2[:], in_=x_sb[:, ti, :], func=AF.Square,
                                                               accum_out=stat[:, 16 + i:17 + i]),
                       reads=[b_x[ti]], writes=[b_junk2, b_stat[7]])
                P.emit(act, lambda e, i=i: e.activation(out=stat[:, 24 + i:25 + i], in_=stat[:, 16 + i:17 + i], func=AF.Ln,
                                                        bias=EPS, scale=1.0 / D), reads=[b_stat[7]], writes=[b_stat[8]])
                P.emit(act, lambda e, i=i: e.activation(out=stat[:, 24 + i:25 + i], in_=stat[:, 24 + i:25 + i], func=AF.Exp,
                                                        scale=-0.5), reads=[b_stat[8]], writes=[b_stat[8]])
                P.emit(dve, lambda e, ti=ti, ob=ob, i=i: e.scalar_tensor_tensor(
                    out=ost[:, ob, :], in0=x_sb[:, ti, :], scalar=stat[:, 24 + i:25 + i], in1=af_bc[:],
                    op0=ALU.mult, op1=ALU.mult), reads=[b_x[ti], b_stat[8], b_condB], writes=[b_ost[ob]])
                P.emit(dve, lambda e, ob=ob: e.tensor_tensor(out=ost[:, ob, :], in0=ost[:, ob, :], in1=sf_bc[:], op=ALU.add),
                       reads=[b_ost[ob], b_condB], writes=[b_ost[ob]])
                P.dma(sp, y_d[ti * 128:(ti + 1) * 128, :], ost[:, ob, :], d_out[ob], reads=[b_ost[ob]])


    except StopBuild:
        pass
    P._wait(sp, [(d.h, d.count) for d in d_out])
    P.barrier()
    for cm in reversed(ctxs):
        cm.__exit__(None, None, None)
    P.close()
    return nc


_NC_CACHE = {}


def _host_layout(inputs):
    f = lambda a: np.ascontiguousarray(np.asarray(a, dtype=np.float32))
    x = f(inputs["x"]); c = f(inputs["c"])
    col = lambda v: f(v).reshape(-1, 128).T
    rowbc = lambda v: np.broadcast_to(f(v).reshape(1, -1), (128, f(v).size))
    ada_b = f(inputs["ada_b"])[0]
    rows = np.empty((128, NROW), np.float32)
    rows[:, R_LNG:R_LNG + 512] = rowbc(inputs["a_ln_g"][0])
    rows[:, R_LNB:R_LNB + 512] = rowbc(inputs["a_ln_b"][0])
    bs = f(inputs["a_spatial_b"])[0]
    rows[:, R_BS:R_BS + 512] = np.repeat(bs.T[:, :, None], 64, axis=2).reshape(128, 512)
    rows[:, R_GF:R_GF + D] = rowbc(inputs["norm_f_g"])
    rows[:, R_ABG1:R_ABG1 + D] = rowbc(ada_b[2 * D:3 * D])
    rows[:, R_ABG2:R_ABG2 + D] = rowbc(ada_b[5 * D:6 * D])
    af_b = f(inputs["ada_f_b"])
    rows[:, R_ABSF:R_ABSF + D] = rowbc(af_b[0:D])
    rows[:, R_ABSCF:R_ABSCF + D] = rowbc(af_b[D:2 * D])
    b_in = f(inputs["b_in"])[0]
    rows[:, R_BINA:R_BINA + D] = rowbc(b_in[0:D])
    wsp = np.ascontiguousarray(f(inputs["a_spatial_w"])[0].transpose(2, 0, 1))
    shared = {
        "rows": rows, "wsp": wsp,
        "ada_w": f(inputs["ada_w"])[0], "ada_f_w": f(inputs["ada_f_w"]),
        "w_in": f(inputs["w_in"])[0], "w_out": f(inputs["w_out"])[0],
        "w_ffn_in": f(inputs["w_ffn_in"])[0], "w_ffn_out": f(inputs["w_ffn_out"])[0],
    }
    base_cols = np.zeros((128, NCOL), np.float32)
    base_cols[:, C_G1:C_G1 + 8] = col(inputs["norm1_g"][0])
    base_cols[:, C_G2:C_G2 + 8] = col(inputs["norm2_g"][0])
    base_cols[:, C_BB:C_BB + 8] = col(b_in[D:2 * D])
    base_cols[:, C_CB:C_CB + 4] = col(inputs["b_conv_b"][0])
    base_cols[:, C_GNG:C_GNG + 4] = col(inputs["b_gn_g"][0])
    base_cols[:, C_GNB:C_GNB + 4] = col(inputs["b_gn_b"][0])
    base_cols[:, C_GA:C_GA + 4] = col(inputs["out_norm_a_g"][0])
    base_cols[:, C_GB:C_GB + 4] = col(inputs["out_norm_b_g"][0])
    base_cols[:, C_AB + 0:C_AB + 8] = col(ada_b[0:D])
    base_cols[:, C_AB + 8:C_AB + 16] = col(ada_b[D:2 * D])
    base_cols[:, C_AB + 16:C_AB + 24] = col(ada_b[3 * D:4 * D])
    base_cols[:, C_AB + 24:C_AB + 32] = col(ada_b[4 * D:5 * D])
    cw = f(inputs["b_conv_w"])[0]
    for cc in range(4):
        base_cols[:, C_CW + cc * 31:C_CW + (cc + 1) * 31] = cw[:, cc * 128:(cc + 1) * 128].T
    in_maps = []
    for core in range(NCORES):
        b, q = core // 4, core % 4
        cols = base_cols.copy()
        cols[:, C_CC:C_CC + 8] = col(c[b])
        cols[:, C_HM] = 0.0 if q == 0 else 1.0
        xs = x[b, q * TOK:(q + 1) * TOK, :]
        if q == 0:
            xh = np.zeros((HALO, D), np.float32)
        else:
            xh = x[b, q * TOK - HALO:q * TOK, :]
        m = {"x": np.ascontiguousarray(xs), "xh": np.ascontiguousarray(xh), "cols": cols}
        m.update(shared)
        in_maps.append(m)
    return in_maps


def kernel(**inputs):
    if "nc" not in _NC_CACHE:
        _NC_CACHE["nc"] = build_nc()
    nc = _NC_CACHE["nc"]
    in_maps = _host_layout(inputs)
    res = run_bass_kernel_spmd(nc, in_maps, core_ids=list(range(NCORES)))
    out = np.empty((2, 4 * TOK, D), np.float32)
    for core in range(NCORES):
        b, q = core // 4, core % 4
        out[b, q * TOK:(q + 1) * TOK, :] = np.asarray(res.results[core]["y"], dtype=np.float32)
    return out
```

```python
import numpy as np
import concourse.bass as bass
import concourse.mybir as mybir
from concourse.bass_utils import run_bass_kernel_spmd

F32 = mybir.dt.float32
BF16 = mybir.dt.bfloat16
AF = mybir.ActivationFunctionType
ALU = mybir.AluOpType
AX = mybir.AxisListType

D = 1024
TOK = 2048
ST = 512
NST = TOK // ST
HALO = 32
DFF = 2816
NJP = 11
EPS = 1e-6
NCORES = 8

C_G1, C_G2, C_BB, C_CB, C_GNG, C_GNB, C_GA, C_GB, C_CC, C_HM = 0, 8, 16, 24, 28, 32, 36, 40, 44, 52
C_AB = 53
C_CW = 85
NCOL = C_CW + 124
R_LNG, R_LNB, R_BS, R_GF, R_ABG1, R_ABG2, R_ABSF, R_ABSCF, R_BINA = 0, 512, 1024, 1536, 2560, 3584, 4608, 5632, 6656
NROW = 7680


import os
KSTOP = int(os.environ.get("KSTOP", "99"))


class StopBuild(Exception):
    pass


def stop_at(n):
    if KSTOP == n:
        raise StopBuild()


class Buf:
    __slots__ = ("w", "r", "name")

    def __init__(self, name=""):
        self.w = None
        self.r = []
        self.name = name


class Eng:
    def __init__(self, name, h, sem, is_pe=False):
        self.name, self.h, self.sem, self.is_pe = name, h, sem, is_pe
        self.count = 0
        self.seen = {}


class DSem:
    def __init__(self, h):
        self.h = h
        self.count = 0


class Prog:
    def __init__(self, nc):
        self.nc = nc
        self._sems = []
        mk = lambda n: self._sem(n)
        self.pe = Eng("pe", nc.tensor, mk("s_pe"), True)
        self.act = Eng("act", nc.scalar, mk("s_act"))
        self.dve = Eng("dve", nc.vector, mk("s_dve"))
        self.pool = Eng("pool", nc.gpsimd, mk("s_pool"))
        self.sp = Eng("sp", nc.sync, mk("s_sp"))
        self.engs = [self.pe, self.act, self.dve, self.pool, self.sp]
        self.dsems = []

    def _sem(self, name):
        cm = self.nc.semaphore(name)
        h = cm.__enter__()
        self._sems.append(cm)
        return h

    def dsem(self, name):
        d = DSem(self._sem(name))
        self.dsems.append(d)
        return d

    def _wait(self, eng, deps):
        need = {}
        for (sem, val) in deps:
            if eng.is_pe and sem is eng.sem:
                continue
            if eng.seen.get(id(sem), 0) >= val:
                continue
            if need.get(id(sem), (None, 0))[1] < val:
                need[id(sem)] = (sem, val)
        for k, (sem, val) in need.items():
            eng.h.wait_ge(sem, val)
            eng.seen[k] = val

    def _deps(self, reads, writes):
        deps = []
        for b in reads:
            if b.w is not None:
                deps.append(b.w)
        for b in writes:
            if b.w is not None:
                deps.append(b.w)
            deps.extend(b.r)
        return deps

    def emit(self, eng, fn, reads=(), writes=(), inc=True):
        self._wait(eng, self._deps(reads, writes))
        ins = fn(eng.h)
        if inc:
            eng.count += 1
            ins.then_inc(eng.sem, 1)
            tok = (eng.sem, eng.count)
        else:
            tok = (eng.sem, eng.count + 1)
        for b in reads:
            b.r.append(tok)
        for b in writes:
            b.w = tok
            b.r = []
        return tok

    def dma(self, q, out_ap, in_ap, dsem, reads=(), writes=()):
        self._wait(q, self._deps(reads, writes))
        ins = q.h.dma_start(out=out_ap, in_=in_ap)
        dsem.count += 16
        ins.then_inc(dsem.h, 16)
        tok = (dsem.h, dsem.count)
        for b in reads:
            b.r.append(tok)
        for b in writes:
            b.w = tok
            b.r = []
        return tok

    def barrier(self):
        toks = [(e.sem, e.count) for e in self.engs if e.count > 0]
        toks += [(d.h, d.count) for d in self.dsems if d.count > 0]
        for e in self.engs:
            self._wait(e, toks)

    def close(self):
        for cm in reversed(self._sems):
            cm.__exit__(None, None, None)


def build_nc():
    nc = bass.Bass("TRN2", target_bir_lowering=False)
    dr = lambda n, shp, kind="ExternalInput": nc.dram_tensor(n, shp, F32, kind=kind).ap()
    x_d = dr("x", [TOK, D])
    xh_d = dr("xh", [HALO, D])
    cols_d = dr("cols", [128, NCOL])
    rows_d = dr("rows", [128, NROW])
    wsp_d = dr("wsp", [128, 8, 128])
    adaw_d = dr("ada_w", [D, 6 * D])
    adafw_d = dr("ada_f_w", [D, 2 * D])
    win_d = dr("w_in", [D, 2 * D])
    wout_d = dr("w_out", [D, D])
    wfi_d = dr("w_ffn_in", [D, 2 * DFF])
    wfo_d = dr("w_ffn_out", [DFF, D])
    y_d = dr("y", [TOK, D], kind="ExternalOutput")
    scr_d = dr("cond_scr", [128, 3 * D], kind="Internal")
    scr_wi = nc.dram_tensor("scr_wi", [6, 128, 8, 2, 512], BF16, kind="Internal").ap()
    scr_wo = nc.dram_tensor("scr_wo", [128, 22, D], BF16, kind="Internal").ap()

    P = Prog(nc)
    pe, act, dve, pool, sp = P.pe, P.act, P.dve, P.pool, P.sp
    ctxs = []

    def sb(name, shape, dt=F32):
        cm = nc.sbuf_tensor("s_" + name, shape, dt)
        t = cm.__enter__()
        ctxs.append(cm)
        return t

    banks = []
    for i in range(8):
        cm = nc.psum_tensor(f"bank{i}", [128, 512], F32)
        banks.append(cm.__enter__())
        ctxs.append(cm)
    bank_bufs = [Buf(f"bank{i}") for i in range(8)]
    rr = [0]

    rr_n = [5]

    def palloc():
        i = rr[0] % rr_n[0]
        rr[0] += 1
        return banks[i], bank_bufs[i]

    cols = sb("cols", [128, NCOL]); b_cols = Buf("cols")
    identf = sb("identf", [128, 128]); b_identf = Buf()
    identb = sb("identb", [128, 128], BF16); b_identb = Buf()
    jb = sb("jb", [128, 128], BF16); b_jb = Buf()
    onesm = sb("onesm", [128, 128], BF16); b_onesm = Buf()
    ones1 = sb("ones1", [1, 128], BF16); b_ones1 = Buf()
    ccols = sb("ccols", [128, 36]); b_ccols = Buf()
    CS1, CA1, CS2, CA2, CCBP = 0, 8, 16, 24, 32
    x_sb = sb("x_sb", [128, 16, D]); b_x = [Buf(f"x{i}") for i in range(16)]
    stat = sb("stat", [128, 64]); b_stat = [Buf(f"stat{i}") for i in range(16)]

    d_cols = P.dsem("d_cols"); d_rowc = P.dsem("d_rowc"); d_stage = P.dsem("d_stage"); d_xh = P.dsem("d_xh")
    d_wspf = P.dsem("d_wspf"); d_wstage = P.dsem("d_wstage"); d_ostage = P.dsem("d_ostage")
    d_x = [P.dsem(f"d_x{i}") for i in range(NST)]
    d_w = P.dsem("d_w")
    d_ring = [P.dsem(f"d_ring{i}") for i in range(4)]
    d_out = [P.dsem(f"d_out{i}") for i in range(2)]
    d_cast = P.dsem("d_cast")

    try:
        P.dma(sp, cols[:], cols_d[:, :], d_cols, writes=[b_cols])

        P.emit(pool, lambda e: e.memset(identf[:], 0.0), writes=[b_identf])
        P.emit(pool, lambda e: e.affine_select(out=identf[:], in_=identf[:], pattern=[[-1, 128]],
                                               compare_op=ALU.not_equal, fill=1.0, base=0, channel_multiplier=1),
               reads=[b_identf], writes=[b_identf])
        P.emit(dve, lambda e: e.tensor_copy(out=identb[:], in_=identf[:]), reads=[b_identf], writes=[b_identb])
        P.emit(pool, lambda e: e.memset(jb[:], 0.0), writes=[b_jb])
        P.emit(pool, lambda e: e.memset(jb[0:64, 0:64], 1.0 / 64), writes=[b_jb])
        P.emit(pool, lambda e: e.memset(jb[64:128, 64:128], 1.0 / 64), writes=[b_jb])
        P.emit(pool, lambda e: e.memset(onesm[:], 1.0 / 512), writes=[b_onesm])
        P.emit(pool, lambda e: e.memset(ones1[:], 1.0), writes=[b_ones1])

        w_in_sb = sb("w_in_sb", [128, 8, 2 * D], BF16); b_win = Buf()
        w_out_sb = sb("w_out_sb", [128, 8, D], BF16); b_wout = [Buf() for _ in range(8)]
        convL = sb("convL", [128, 4, 31, 128], BF16); b_convL = Buf()
        wct = sb("wct", [128, 8, 128], BF16); b_wct = Buf()
        lng_bc = sb("lng_bc", [128, 512]); lnb_bc = sb("lnb_bc", [128, 512]); bs_full = sb("bs_full", [128, 512])
        b_rowc = Buf()
        binhi = sb("binhi", [1, D], BF16); b_bin = Buf()
        caH = sb("caH", [128, 8, 128], BF16); b_caH = Buf()
        tmpR = sb("tmpR", [128, 256]); b_tmpR = Buf()
        stage = sb("stage", [128, 512]); b_stage = Buf()
        xn = sb("xn", [128, 2, D], BF16); b_xn = [Buf(), Buf()]
        sq2 = sb("sq2", [128, 2, 512], BF16)
        dsq = sq2[:, 0, :]; ybsq = sq2[:, 1, :]; junk = sq2[:].rearrange("p a b -> p (a b)")
        junkx = sb("junkx", [128, D], BF16); b_junkx = Buf()
        b_dsq = Buf(); b_ybsq = Buf()
        actT = sb("actT", [128, 8, ST], BF16); b_actT = [Buf(f"actT{k}") for k in range(8)]
        gu4 = sb("gu4", [128, 4, 512], BF16); b_gu = [Buf() for _ in range(4)]
        gv4 = sb("gv4", [128, 4, 512], BF16); b_gv = [Buf() for _ in range(4)]
        vln = sb("vln", [128, 512], BF16); b_vln = Buf()
        ya = sb("ya", [128, 512]); b_ya = Buf()
        yan2 = sb("yan2", [128, 2, 512], BF16); b_yan2 = [Buf(), Buf()]
        ybuf = sb("ybuf", [128, 4, HALO + ST], BF16); b_ybuf = [Buf() for _ in range(4)]
        sig = sb("sig", [128, 512]); b_sig = Buf()
        vtmp = sig; b_vtmp = b_sig
        lnv = sb("lnv", [128, 512]); b_lnv = Buf()
        dsq2 = sb("dsq2", [128, 2, 512], BF16); b_dsq2 = [Buf(), Buf()]
        cnb = sb("cnb", [128, 512]); b_cnb = Buf()

        P.dma(sp, lng_bc[:], rows_d[:, R_LNG:R_LNG + 512], d_rowc, writes=[b_rowc])
        P.dma(sp, lnb_bc[:], rows_d[:, R_LNB:R_LNB + 512], d_rowc, writes=[b_rowc])
        P.dma(sp, bs_full[:], rows_d[:, R_BS:R_BS + 512], d_rowc, writes=[b_rowc])
        P.dma(sp, stage[0:1, 0:256], rows_d[0:1, R_BINA:R_BINA + 256], d_stage, writes=[b_stage])
        for q in range(4):
            if q > 0:
                P.dma(sp, stage[0:1, 0:256], rows_d[0:1, R_BINA + 256 * q:R_BINA + 256 * (q + 1)], d_stage, writes=[b_stage])
            P.emit(dve, lambda e, q=q: e.tensor_copy(out=binhi[0:1, 256 * q:256 * (q + 1)], in_=stage[0:1, 0:256]),
                   reads=[b_stage], writes=[b_bin])
        P.dma(sp, x_sb[:, 0:4, :], x_d[0:ST, :].rearrange("(i p) d -> p i d", p=128), d_x[0], writes=b_x[0:4])
        xh_t = x_sb[0:HALO, 4, :]; b_xh = b_x[4]
        P.dma(sp, xh_t, xh_d[:, :], d_xh, writes=[b_xh])
        wspf = x_sb[:, 5, :].rearrange("p (h t) -> p h t", h=8); b_wspf = b_x[5]
        P.dma(sp, wspf, wsp_d[:, :, :], d_wspf, writes=[b_wspf])
        P.emit(pool, lambda e: e.affine_select(out=wspf, in_=wspf, pattern=[[0, 8], [1, 128]],
                                               compare_op=ALU.is_ge, fill=0.0, base=0, channel_multiplier=-1),
               reads=[b_wspf], writes=[b_wspf])
        P.emit(dve, lambda e: e.tensor_copy(out=wct[:], in_=wspf), reads=[b_wspf], writes=[b_wct])
        bmat = sb("bmat", [128, 128]); b_bmat = Buf()
        P.emit(dve, lambda e: e.tensor_tensor(out=bmat[:], in0=identf[:], in1=jb[:], op=ALU.subtract),
               reads=[b_identf, b_jb], writes=[b_bmat])
        b_convLc = [Buf() for _ in range(4)]

        def gen_convL():
            for cc in range(4):
                for k in range(31):
                    col = cols[:, C_CW + cc * 31 + k:C_CW + cc * 31 + k + 1]
                    if cc % 2 == 0:
                        P.emit(dve, lambda e, cc=cc, k=k, col=col: e.tensor_scalar(
                            out=convL[:, cc, k, :], in0=bmat[:], scalar1=col, scalar2=None, op0=ALU.mult),
                            reads=[b_bmat, b_cols], writes=[b_convLc[cc]])
                    else:
                        P.emit(act, lambda e, cc=cc, k=k, col=col: e.activation(
                            out=convL[:, cc, k, :], in_=bmat[:], func=AF.Identity, scale=col),
                            reads=[b_bmat, b_cols], writes=[b_convLc[cc]])

        cbb = sb("cbb", [128, 4], BF16); b_cbb = Buf()
        P.emit(dve, lambda e: e.tensor_copy(out=cbb[:], in_=cols[:, C_CB:C_CB + 4]), reads=[b_cols], writes=[b_cbb])
        bk, bb = palloc()
        P.emit(pe, lambda e: e.matmul(bk[:, 0:4], lhsT=jb[:], rhs=cbb[:], start=True, stop=True),
               reads=[b_jb, b_cbb], writes=[bb])
        P.emit(dve, lambda e: e.tensor_tensor(out=ccols[:, CCBP:CCBP + 4], in0=cols[:, C_CB:C_CB + 4], in1=bk[:, 0:4],
                                              op=ALU.subtract), reads=[b_cols, bb], writes=[b_ccols])
        caf = sb("caf", [128, 8]); b_caf = Buf()
        cab = sb("cab", [128, 8], BF16); b_cab = Buf()
        P.emit(act, lambda e: e.activation(out=caf[:], in_=cols[:, C_CC:C_CC + 8], func=AF.Silu),
               reads=[b_cols], writes=[b_caf])
        P.emit(dve, lambda e: e.tensor_copy(out=cab[:], in_=caf[:]), reads=[b_caf], writes=[b_cab])
        for k in range(8):
            P.emit(dve, lambda e, k=k: e.tensor_copy(out=caH[:, k, :], in_=cab[:, k:k + 1].to_broadcast([128, 128])),
                   reads=[b_cab], writes=[b_caH])

        gate1_bc = x_sb[:, 6, :]; b_g1bc = b_x[6]
        wstage = x_sb[:, 7, :]; b_wstage = b_x[7]
        ostage = sb("ostage", [128, 512]); b_ostage = Buf()

        ring_views = []
        ring_bufs = []
        for j in range(4):
            v = x_sb[:, 8 + 2 * j:10 + 2 * j, :].rearrange("p a d -> p (a d)").bitcast(BF16)
            ring_views.append(v.rearrange("p (k n) -> p k n", k=8))
            ring_bufs.append([b_x[8 + 2 * j], b_x[9 + 2 * j]])
        ada_dma_next = [0]

        def ada_slot(b):
            return b % 4 if b < 12 else 2 + (b % 2)

        def adaln_dma(upto):
            while ada_dma_next[0] <= min(upto, 15):
                b = ada_dma_next[0]
                if b < 12:
                    src = adaw_d[:, b * 512:(b + 1) * 512]
                else:
                    src = adafw_d[:, (b - 12) * 512:(b - 11) * 512]
                sl_ = ada_slot(b)
                P.dma(pool, ring_views[sl_], src.rearrange("(k p) n -> p k n", p=128), d_ring[sl_],
                      writes=ring_bufs[sl_])
                ada_dma_next[0] += 1

        def adaln_block(b):
            kind = b // 2
            q = b % 2
            adaln_dma(b)
            ring = ring_views[ada_slot(b)]
            rb = ring_bufs[ada_slot(b)]
            bk, bb = palloc()
            for k in range(8):
                P.emit(pe, lambda e, k=k: e.matmul(bk[:], lhsT=caH[:, k, :], rhs=ring[:, k, :],
                                                   start=(k == 0), stop=(k == 7)),
                       reads=[b_caH] + rb, writes=[bb], inc=(k == 7))
            adaln_dma(min(b + 3, 11) if b + 1 < 12 else b + 1)
            if kind in (0, 1, 3, 4):
                for h in range(4):
                    ch = q * 4 + h
                    P.emit(dve, lambda e, h=h: e.tensor_tensor(out=tmpR[:, 0:128], in0=bk[:, h * 128:(h + 1) * 128],
                                                              in1=identf[:], op=ALU.mult),
                           reads=[bb, b_identf], writes=[b_tmpR])
                    P.emit(dve, lambda e: e.tensor_reduce(out=tmpR[:, 128:129], in_=tmpR[:, 0:128], axis=AX.X, op=ALU.add),
                           reads=[b_tmpR], writes=[b_tmpR])
                    abc = C_AB + {0: 0, 1: 8, 3: 16, 4: 24}[kind] + ch
                    if kind in (0, 3):
                        dst = (CS1 if kind == 0 else CS2) + ch
                        P.emit(dve, lambda e, dst=dst, abc=abc: e.tensor_tensor(
                            out=ccols[:, dst:dst + 1], in0=tmpR[:, 128:129], in1=cols[:, abc:abc + 1], op=ALU.add),
                            reads=[b_tmpR, b_cols], writes=[b_ccols])
                    else:
                        dst = (CA1 if kind == 1 else CA2) + ch
                        gcol = (C_G1 if kind == 1 else C_G2) + ch
                        P.emit(dve, lambda e, abc=abc: e.tensor_scalar(
                            out=tmpR[:, 129:130], in0=tmpR[:, 128:129], scalar1=cols[:, abc:abc + 1], scalar2=1.0,
                            op0=ALU.add, op1=ALU.add), reads=[b_tmpR, b_cols], writes=[b_tmpR])
                        P.emit(dve, lambda e, dst=dst, gcol=gcol: e.tensor_tensor(
                            out=ccols[:, dst:dst + 1], in0=tmpR[:, 129:130], in1=cols[:, gcol:gcol + 1], op=ALU.mult),
                            reads=[b_tmpR, b_cols], writes=[b_ccols])
            else:
                roff = {2: R_ABG1, 5: R_ABG2, 6: R_ABSF, 7: R_ABSCF}[kind] + q * 512
                P.dma(sp, stage[:], rows_d[:, roff:roff + 512], d_stage, writes=[b_stage])
                if kind == 2:
                    P.emit(dve, lambda e: e.tensor_tensor(out=gate1_bc[:, q * 512:(q + 1) * 512], in0=bk[:],
                                                          in1=stage[:], op=ALU.add),
                           reads=[bb, b_stage], writes=[b_g1bc])
                else:
                    P.emit(dve, lambda e: e.tensor_tensor(out=ostage[:], in0=bk[:], in1=stage[:], op=ALU.add),
                           reads=[bb, b_stage], writes=[b_ostage])
                    if kind == 7:
                        P.dma(sp, stage[:], rows_d[:, R_GF + q * 512:R_GF + (q + 1) * 512], d_stage, writes=[b_stage])
                        P.emit(dve, lambda e: e.scalar_tensor_tensor(out=ostage[:], in0=ostage[:], scalar=1.0, in1=stage[:],
                                                                     op0=ALU.add, op1=ALU.mult),
                               reads=[b_ostage, b_stage], writes=[b_ostage])
                    so = {5: 0, 6: D, 7: 2 * D}[kind] + q * 512
                    P.dma(sp, scr_d[:, so:so + 512], ostage[:], d_ostage, reads=[b_ostage])

        adaln_dma(3)
        d_w2 = P.dsem("d_w2")
        b_winA = Buf()
        P.dma(pool, w_in_sb[:, :, D:2 * D], win_d[:, D:2 * D].rearrange("(k p) n -> p k n", p=128), d_w, writes=[b_win])
        P.dma(pool, w_in_sb[:, :, 0:D], win_d[:, 0:D].rearrange("(k p) n -> p k n", p=128), d_w2, writes=[b_winA])
        stop_at(1)
        for b in range(4):
            adaln_block(b)
        stop_at(2)

        def wout_prep():
            for k in range(8):
                P.dma(sp, wstage, wout_d[k * 128:(k + 1) * 128, :], d_wstage, writes=[b_wstage])
                P.emit(pool, lambda e, k=k: e.tensor_tensor(out=wstage, in0=wstage, in1=gate1_bc, op=ALU.mult),
                       reads=[b_wstage, b_g1bc], writes=[b_wstage])
                gcol = (C_GA + k) if k < 4 else (C_GB + k - 4)
                P.emit(pool, lambda e, k=k, gcol=gcol: e.tensor_tensor(out=w_out_sb[:, k, :], in0=wstage,
                                                                      in1=cols[:, gcol:gcol + 1].to_broadcast([128, D]),
                                                                      op=ALU.mult),
                       reads=[b_wstage, b_cols], writes=[b_wout[k]])

        def rstd_from(sum_ap, out_ap, n, dim, rbufs, wbufs):
            P.emit(act, lambda e: e.activation(out=out_ap, in_=sum_ap, func=AF.Ln, bias=EPS, scale=1.0 / dim),
                   reads=rbufs, writes=wbufs)
            P.emit(act, lambda e: e.activation(out=out_ap, in_=out_ap, func=AF.Exp, scale=-0.5),
                   reads=wbufs, writes=wbufs)

        def norm_transpose(src_tiles, src_bufs, ntok, acol, scol, sbuf_stat, dst, dst_bufs, tw, phase=0):
            nt = len(src_tiles)
            ss = stat[0:tw, 0:nt]
            rs = stat[0:tw, 8:8 + nt]
            def do_xn(i):
                t, tb = src_tiles[i], src_bufs[i]
                xb = i % 2
                P.emit(dve, lambda e: e.tensor_scalar(out=xn[0:tw, xb, :], in0=t,
                                                      scalar1=stat[0:tw, 8 + i:9 + i], scalar2=None,
                                                      op0=ALU.mult),
                       reads=[tb, sbuf_stat], writes=[b_xn[xb]])

            if phase in (0, 1):
                for i, (t, tb) in enumerate(zip(src_tiles, src_bufs)):
                    P.emit(act, lambda e, t=t, i=i: e.activation(out=junkx[0:tw, :], in_=t, func=AF.Square,
                                                                 accum_out=stat[0:tw, i:i + 1]),
                           reads=[tb], writes=[b_junkx, sbuf_stat])
                rstd_from(ss, rs, nt, D, [sbuf_stat], [sbuf_stat])
                for i in range(min(2, nt)):
                    do_xn(i)
            if phase == 1:
                return
            pbs = [palloc() for _ in range(4)]
            for i, (t, tb) in enumerate(zip(src_tiles, src_bufs)):
                xb = i % 2
                if i >= 2:
                    do_xn(i)
                for k in range(8):
                    bk, bb = pbs[k // 2]
                    pv = bk[:].bitcast(BF16)
                    c0 = (k % 2) * 512 + i * tw
                    P.emit(pe, lambda e, pv=pv, c0=c0, k=k, xb=xb: e.transpose(
                        out=pv[:, c0:c0 + tw], in_=xn[0:tw, xb, k * 128:(k + 1) * 128], identity=identb[0:tw, 0:tw]),
                        reads=[b_xn[xb], b_identb], writes=[bb], inc=(k == 7))
            if tw == 128: stop_at(32)
            n = nt * tw
            for k in range(8):
                bk, bb = pbs[k // 2]
                pv = bk[:].bitcast(BF16)
                c0 = (k % 2) * 512
                eng = act if (k // 2) % 2 == 0 else dve
                if os.environ.get('KEV') == 'act': eng = act
                if os.environ.get('KEV') == 'dve': eng = dve
                if eng is act:
                    P.emit(act, lambda e, pv=pv, c0=c0, k=k: e.activation(
                        out=dst[:, k, 0:n], in_=pv[:, c0:c0 + n], func=AF.Identity,
                        bias=ccols[:, scol + k:scol + k + 1], scale=ccols[:, acol + k:acol + k + 1]),
                        reads=[bb, b_ccols], writes=[dst_bufs[k]])
                else:
                    P.emit(dve, lambda e, pv=pv, c0=c0, k=k: e.tensor_scalar(
                        out=dst[:, k, 0:n], in0=pv[:, c0:c0 + n], scalar1=ccols[:, acol + k:acol + k + 1],
                        scalar2=ccols[:, scol + k:scol + k + 1], op0=ALU.mult, op1=ALU.add),
                        reads=[bb, b_ccols], writes=[dst_bufs[k]])

        def b_branch_y(n, col0, mask):
            for cc in range(4):
                bv, bbv = palloc()
                bg, bbg = palloc()
                for (bk, bb, coff) in ((bv, bbv, D + cc * 128), (bg, bbg, D + 512 + cc * 128)):
                    for k in range(8):
                        P.emit(pe, lambda e, bk=bk, k=k, coff=coff: e.matmul(
                            bk[:, 0:n], lhsT=w_in_sb[:, k, coff:coff + 128], rhs=actT[:, k, 0:n],
                            start=(k == 0), stop=(k == 7)),
                            reads=[b_win, b_actT[k]], writes=[bb], inc=(k == 7))
                P.emit(act, lambda e, bg=bg, cc=cc: e.activation(out=sig[:, 0:n], in_=bg[:, 0:n], func=AF.Sigmoid,
                                                                bias=cols[:, C_BB + 4 + cc:C_BB + 5 + cc]),
                       reads=[bbg, b_cols], writes=[b_sig])
                if mask:
                    P.emit(dve, lambda e: e.tensor_scalar(out=sig[:, 0:n], in0=sig[:, 0:n], scalar1=cols[:, C_HM:C_HM + 1],
                                                          scalar2=None, op0=ALU.mult),
                           reads=[b_sig, b_cols], writes=[b_sig])
                P.emit(dve, lambda e, bv=bv, cc=cc: e.scalar_tensor_tensor(
                    out=ybuf[:, cc, col0:col0 + n], in0=bv[:, 0:n], scalar=cols[:, C_BB + cc:C_BB + cc + 1],
                    in1=sig[:, 0:n], op0=ALU.add, op1=ALU.mult),
                    reads=[bbv, b_cols, b_sig], writes=[b_ybuf[cc]])

        stop_at(3)
        norm_transpose([xh_t], [b_xh], HALO, CA1, CS1, b_stat[1], actT, b_actT, HALO)
        b_branch_y(HALO, 0, True)

        stop_at(4)
        ada_next = [4]

        def ada_fill(n):
            for _ in range(n):
                if ada_next[0] < 16:
                    adaln_block(ada_next[0])
                    ada_next[0] += 1

        def prefetch_x(sn):
            P.dma(sp, x_sb[:, 4 * sn:4 * (sn + 1), :],
                  x_d[sn * ST:(sn + 1) * ST, :].rearrange("(i p) d -> p i d", p=128), d_x[sn],
                  writes=b_x[4 * sn:4 * (sn + 1)])

        def front_tiles(sn):
            return [x_sb[:, 4 * sn + i, :] for i in range(4)], b_x[4 * sn:4 * sn + 4]

        t0_, tb0_ = front_tiles(0)
        norm_transpose(t0_, tb0_, 128, CA1, CS1, b_stat[1], actT, b_actT, 128, phase=1)
        for s in range(NST):
            if 1 <= s and s + 1 < NST:
                assert ada_next[0] >= (12 if s == 1 else 16)
                prefetch_x(s + 1)
            tiles, tbufs = front_tiles(s)
            norm_transpose(tiles, tbufs, 128, CA1, CS1, b_stat[1], actT, b_actT, 128, phase=2)
            if s == 0:
                gen_convL()
            for i in range(4):
                for half, (dst, dbuf) in enumerate(((gu4, b_gu[i]), (gv4, b_gv[i]))):
                    bk, bb = palloc()
                    for k in range(8):
                        P.emit(pe, lambda e, bk=bk, k=k, i=i, half=half: e.matmul(
                            bk[:], lhsT=actT[:, k, i * 128:(i + 1) * 128], rhs=w_in_sb[:, k, half * 512:(half + 1) * 512],
                            start=(k == 0), stop=False), reads=[b_actT[k], b_winA], writes=[bb], inc=False)
                    P.emit(pe, lambda e, bk=bk, half=half: e.matmul(
                        bk[:], lhsT=ones1[0:1, :], rhs=binhi[0:1, half * 512:(half + 1) * 512], start=False, stop=True),
                        reads=[b_ones1, b_bin], writes=[bb])
                    P.emit(act, lambda e, bk=bk, dst=dst, i=i: e.activation(out=dst[:, i, :], in_=bk[:], func=AF.Gelu),
                           reads=[bb], writes=[dbuf])
                P.emit(dve, lambda e, i=i: e.bn_stats(out=stat[:, 16 + 6 * i:22 + 6 * i], in_=gv4[:, i, :]),
                       reads=[b_gv[i]], writes=[b_stat[2]])
                P.emit(dve, lambda e, i=i: e.bn_aggr(out=stat[:, 40 + 2 * i:42 + 2 * i], in_=stat[:, 16 + 6 * i:22 + 6 * i]),
                       reads=[b_stat[2]], writes=[b_stat[3]])
            if s == 0:
                ada_fill(2)
                wout_prep()
                prefetch_x(1)
                ada_fill(1)
            if s == 1:
                ada_fill(1)
            b_branch_y(ST, HALO, False)
            mv = stat[:, 40:48].rearrange("p (i t) -> p i t", t=2)
            P.emit(act, lambda e: e.activation(out=stat[:, 48:52], in_=mv[:, :, 1], func=AF.Ln, bias=EPS, scale=1.0),
                   reads=[b_stat[3]], writes=[b_stat[4]])
            P.emit(act, lambda e: e.activation(out=stat[:, 48:52], in_=stat[:, 48:52], func=AF.Exp, scale=-0.5),
                   reads=[b_stat[4]], writes=[b_stat[4]])
            if s == 0:
                ada_fill(2)
            if s == 1:
                ada_fill(1)

            bm, bbm = banks[7], bank_bufs[7]
            dbank = {}
            sbank = {}
            ptA = []

            def conv(cc):
                bi_ = (0, 1, 4)[cc % 3]
                bd, bbd = banks[bi_], bank_bufs[bi_]
                dbank[cc] = (bd, bbd)
                for k in range(31):
                    P.emit(pe, lambda e, k=k: e.matmul(
                        bd[:], lhsT=convL[:, cc, k, :], rhs=ybuf[:, cc, HALO - 30 + k:HALO - 30 + k + ST],
                        start=(k == 0), stop=(k == 30)), reads=[b_convLc[cc], b_ybuf[cc]], writes=[bbd], inc=(k == 30))
                P.emit(act, lambda e: e.activation(out=dsq2[:, cc % 2, :], in_=bd[:], func=AF.Square,
                                                   bias=ccols[:, CCBP + cc:CCBP + cc + 1]),
                       reads=[bbd, b_ccols], writes=[b_dsq2[cc % 2]])
                P.emit(dve, lambda e: e.tensor_copy(out=ybuf[:, cc, 0:HALO], in_=ybuf[:, cc, ST:ST + HALO]),
                       reads=[b_ybuf[cc]], writes=[b_ybuf[cc]])

            def gn_var(cc):
                bd, bbd = dbank[cc]
                bvv, bbvv = banks[2], bank_bufs[2]
                P.emit(pe, lambda e: e.matmul(bvv[:], lhsT=jb[:], rhs=dsq2[:, cc % 2, :], start=True, stop=True),
                       reads=[b_jb, b_dsq2[cc % 2]], writes=[bbvv])
                P.emit(act, lambda e: e.activation(out=lnv[:], in_=bvv[:], func=AF.Ln, bias=EPS, scale=1.0),
                       reads=[bbvv], writes=[b_lnv])
                P.emit(act, lambda e: e.activation(out=lnv[:], in_=lnv[:], func=AF.Exp, scale=-0.5),
                       reads=[b_lnv], writes=[b_lnv])
                P.emit(dve, lambda e: e.scalar_tensor_tensor(out=cnb[:], in0=bd[:], scalar=ccols[:, CCBP + cc:CCBP + cc + 1],
                                                             in1=lnv[:], op0=ALU.add, op1=ALU.mult),
                       reads=[bbd, b_ccols, b_lnv], writes=[b_cnb])
                P.emit(act, lambda e: e.activation(out=actT[:, 4 + cc, :], in_=cnb[:], func=AF.Silu,
                                                   bias=cols[:, C_GNB + cc:C_GNB + cc + 1],
                                                   scale=cols[:, C_GNG + cc:C_GNG + cc + 1]),
                       reads=[b_cnb, b_cols], writes=[b_actT[4 + cc]])
                P.emit(act, lambda e: e.activation(out=ybsq, in_=actT[:, 4 + cc, :], func=AF.Square),
                       reads=[b_actT[4 + cc]], writes=[b_ybsq])

            def gn_msq(cc):
                for i in range(4):
                    P.emit(pe, lambda e, i=i: e.matmul(bm[:, i * 4 + cc:i * 4 + cc + 1], lhsT=ybsq[:, i * 128:(i + 1) * 128],
                                                       rhs=onesm[:, 0:1], start=True, stop=True),
                           reads=[b_onesm, b_ybsq], writes=[bbm], inc=(i == 3))

            def a_pre(i):
                P.emit(dve, lambda e: e.tensor_scalar(out=vtmp[:], in0=gv4[:, i, :], scalar1=stat[:, 40 + 2 * i:41 + 2 * i],
                                                      scalar2=stat[:, 48 + i:49 + i], op0=ALU.subtract, op1=ALU.mult),
                       reads=[b_gv[i], b_stat[3], b_stat[4]], writes=[b_vtmp])
                P.emit(dve, lambda e: e.tensor_tensor(out=vtmp[:], in0=vtmp[:], in1=lng_bc[:], op=ALU.mult),
                       reads=[b_vtmp, b_rowc], writes=[b_vtmp])
                P.emit(dve, lambda e: e.tensor_tensor(out=vln[:], in0=vtmp[:], in1=lnb_bc[:], op=ALU.add),
                       reads=[b_vtmp, b_rowc], writes=[b_vln])
                bk, bb = banks[3], bank_bufs[3]
                sbank[i] = (bk, bb)
                for h in range(8):
                    P.emit(pe, lambda e, h=h: e.matmul(bk[:, h * 64:(h + 1) * 64], lhsT=wct[:, h, :],
                                                       rhs=vln[:, h * 64:(h + 1) * 64], start=True, stop=True),
                           reads=[b_wct, b_vln], writes=[bb], inc=(h == 7))

            def a_post(i):
                bk, bb = sbank[i]
                yb_ = i % 2
                P.emit(dve, lambda e: e.tensor_tensor(out=ya[:], in0=bk[:], in1=bs_full[:], op=ALU.add),
                       reads=[bb, b_rowc], writes=[b_ya])
                P.emit(dve, lambda e: e.tensor_tensor(out=yan2[:, yb_, :], in0=ya[:], in1=gu4[:, i, :], op=ALU.mult),
                       reads=[b_ya, b_gu[i]], writes=[b_yan2[yb_]])
                P.emit(act, lambda e: e.activation(out=junk[:, 0:512], in_=yan2[:, yb_, :], func=AF.Square,
                                                   accum_out=stat[:, 52 + i:53 + i]),
                       reads=[b_yan2[yb_]], writes=[b_dsq, b_stat[5]])
                if i == 0:
                    ptA.extend([(banks[5], bank_bufs[5]), (banks[6], bank_bufs[6])])
                for c in range(4):
                    bk2, bb2 = ptA[c // 2]
                    pv = bk2[:].bitcast(BF16)
                    c0 = (c % 2) * 512 + i * 128
                    P.emit(pe, lambda e, pv=pv, c0=c0, c=c: e.transpose(out=pv[:, c0:c0 + 128],
                                                                       in_=yan2[:, yb_, c * 128:(c + 1) * 128],
                                                                       identity=identb[:]),
                           reads=[b_yan2[yb_], b_identb], writes=[bb2], inc=(c == 3))

            conv(0)
            a_pre(0)
            conv(1)
            a_post(0)
            a_pre(1)
            gn_var(0)
            conv(2)
            a_post(1)
            a_pre(2)
            gn_msq(0)
            gn_var(1)
            conv(3)
            a_post(2)
            a_pre(3)
            gn_msq(1)
            gn_var(2)
            a_post(3)
            rstd_from(stat[:, 52:56], stat[:, 56:60], 4, 512, [b_stat[5]], [b_stat[6]])
            for c in range(4):
                bk2, bb2 = ptA[c // 2]
                pv = bk2[:].bitcast(BF16)
                c0 = (c % 2) * 512
                P.emit(dve, lambda e, pv=pv, c0=c0, c=c: e.tensor_copy(out=actT[:, c, :], in_=pv[:, c0:c0 + 512]),
                       reads=[bb2], writes=[b_actT[c]])
            gn_msq(2)
            gn_var(3)
            for i in range(4):
                for hf in range(2):
                    pa, bpa = palloc()
                    for k in range(4):
                        P.emit(pe, lambda e, k=k: e.matmul(
                            pa[:], lhsT=actT[:, k, i * 128:(i + 1) * 128], rhs=w_out_sb[:, k, hf * 512:(hf + 1) * 512],
                            start=(k == 0), stop=(k == 3)), reads=[b_actT[k], b_wout[k]], writes=[bpa], inc=(k == 3))
                    xt = x_sb[:, 4 * s + i, hf * 512:(hf + 1) * 512]
                    P.emit(dve, lambda e: e.scalar_tensor_tensor(out=xt, in0=pa[:], scalar=stat[:, 56 + i:57 + i], in1=xt,
                                                                 op0=ALU.mult, op1=ALU.add),
                           reads=[bpa, b_stat[6], b_x[4 * s + i]], writes=[b_x[4 * s + i]])
            if s == 0:
                ada_fill(3)
            if s == 1:
                ada_fill(2)
                assert ada_next[0] >= 16
                for q_ in range(6):
                    c0_, w_ = (q_ * 512, 512) if q_ < 5 else (2560, 256)
                    for g_ in range(2):
                        P.dma(pool, scr_wi[q_, :, :, g_, 0:w_],
                              wfi_d[:, g_ * DFF + c0_:g_ * DFF + c0_ + w_].rearrange("(k p) n -> p k n", p=128), d_cast)
                for h_ in range(2):
                    P.dma(pool, scr_wo[:, 11 * h_:11 * (h_ + 1), :],
                          wfo_d[11 * h_ * 128:11 * (h_ + 1) * 128, :].rearrange("(j p) n -> p j n", p=128), d_cast)
            if s + 1 < NST:
                tn, tbn = front_tiles(s + 1)
                norm_transpose(tn, tbn, 128, CA1, CS1, b_stat[1], actT, b_actT, 128, phase=1)
            for i in range(4):
                for hf in range(2):
                    pb_, bpb = palloc()
                    for k in range(4, 8):
                        P.emit(pe, lambda e, k=k: e.matmul(
                            pb_[:], lhsT=actT[:, k, i * 128:(i + 1) * 128], rhs=w_out_sb[:, k, hf * 512:(hf + 1) * 512],
                            start=(k == 4), stop=(k == 7)), reads=[b_actT[k], b_wout[k]], writes=[bpb], inc=(k == 7))
                    if i == 0 and hf == 0:
                        gn_msq(3)
                        P.emit(dve, lambda e: e.tensor_reduce(out=stat[:, 60:64],
                                                              in_=bm[:, 0:16].rearrange("p (i c) -> p i c", c=4),
                                                              axis=AX.X, op=ALU.add), reads=[bbm], writes=[b_stat[9]])
                        rstd_from(stat[:, 60:64], stat[:, 60:64], 4, 1, [b_stat[9]], [b_stat[9]])
                    xt = x_sb[:, 4 * s + i, hf * 512:(hf + 1) * 512]
                    P.emit(dve, lambda e: e.scalar_tensor_tensor(out=xt, in0=pb_[:], scalar=stat[:, 60 + i:61 + i], in1=xt,
                                                                 op0=ALU.mult, op1=ALU.add),
                           reads=[bpb, b_stat[9], b_x[4 * s + i]], writes=[b_x[4 * s + i]])
            stop_at(5 + s)
        assert ada_next[0] >= 16
        stop_at(9)
        rr_n[0] = 8
        P.barrier()
        nA = len(ctxs)
        keep = 8 + 9
        for cm in reversed(ctxs[keep:]):
            cm.__exit__(None, None, None)
        del ctxs[keep:]

        gate2_bc = sb("gate2_bc", [128, D]); af_bc = sb("af_bc", [128, D]); sf_bc = sb("sf_bc", [128, D])
        b_condB = Buf()
        h2T = sb("h2T", [128, 8, ST], BF16); b_h2T = [Buf() for _ in range(8)]
        aT = sb("aT", [128, 22, ST], BF16); b_aT = [Buf() for _ in range(22)]
        wo_sb = sb("wo_sb", [128, 22, D], BF16); b_wo = Buf()
        wi_ring = sb("wi_ring", [128, 2, 8, 2, 512], BF16); b_wi = [Buf() for _ in range(2)]
        xn2 = sb("xn2", [128, 2, D], BF16); b_xn2 = [Buf(), Buf()]
        junk2 = sb("junk2", [128, D], BF16); b_junk2 = Buf()
        sg = sb("sg", [128, 2, 512]); b_sg = [Buf(), Buf()]
        ost = sb("ost", [128, 2, D]); b_ost = [Buf(), Buf()]
        d_cb = P.dsem("d_cb")
        d_wo = P.dsem("d_wo")
        d_wi = [P.dsem(f"d_wi{i}") for i in range(2)]

        P.dma(sp, gate2_bc[:], scr_d[:, 0:D], d_cb, writes=[b_condB])
        P.dma(sp, sf_bc[:], scr_d[:, D:2 * D], d_cb, writes=[b_condB])
        P.dma(sp, af_bc[:], scr_d[:, 2 * D:3 * D], d_cb, writes=[b_condB])

        NLD = 6
        NL = NST * NLD
        issued = [0]
        wfi_v = wfi_d.rearrange("(k p) (g c) -> p k g c", p=128, g=2)
        wo_loaded = [False]

        def load_cols(q):
            return (q * 512, 512) if q < 5 else (2560, 256)

        def wi_ensure(upto):
            while issued[0] <= min(upto, NL - 1):
                g = issued[0]
                q_ = g % NLD
                c0, w = load_cols(q_)
                sl = g % 2
                P.dma(sp, wi_ring[:, sl, :, :, 0:w], scr_wi[q_, :, :, :, 0:w], d_wi[sl], writes=[b_wi[sl]])
                issued[0] += 1
                if g == 1 and not wo_loaded[0]:
                    wo_loaded[0] = True
                    P.dma(sp, wo_sb[:], scr_wo[:, :, :], d_wo, writes=[b_wo])

        wi_ensure(1)

        def norm_transpose_b(s, phase=0):
            ss = stat[:, 0:4]
            rs = stat[:, 8:12]

            def do_xn2(i):
                xb = i % 2
                P.emit(dve, lambda e: e.tensor_scalar(out=xn2[:, xb, :], in0=x_sb[:, 4 * s + i, :],
                                                      scalar1=stat[:, 8 + i:9 + i], scalar2=None, op0=ALU.mult),
                       reads=[b_x[4 * s + i], b_stat[1]], writes=[b_xn2[xb]])

            for i in (range(4) if phase in (0, 1) else ()):
                P.emit(act, lambda e, i=i: e.activation(out=junk2[:], in_=x_sb[:, 4 * s + i, :], func=AF.Square,
                                                        accum_out=stat[:, i:i + 1]),
                       reads=[b_x[4 * s + i]], writes=[b_junk2, b_stat[1]])
            if phase in (0, 1):
                P.emit(act, lambda e: e.activation(out=rs, in_=ss, func=AF.Ln, bias=EPS, scale=1.0 / D),
                       reads=[b_stat[1]], writes=[b_stat[1]])
                P.emit(act, lambda e: e.activation(out=rs, in_=rs, func=AF.Exp, scale=-0.5),
                       reads=[b_stat[1]], writes=[b_stat[1]])
                do_xn2(0)
                do_xn2(1)
            if phase == 1:
                return
            pbs = [palloc() for _ in range(4)]
            for i in range(4):
                xb = i % 2
                if i >= 2:
                    do_xn2(i)
                for k in range(8):
                    bk, bb = pbs[k // 2]
                    pv = bk[:].bitcast(BF16)
                    c0 = (k % 2) * 512 + i * 128
                    P.emit(pe, lambda e, pv=pv, c0=c0, k=k, xb=xb: e.transpose(
                        out=pv[:, c0:c0 + 128], in_=xn2[:, xb, k * 128:(k + 1) * 128], identity=identb[:]),
                        reads=[b_xn2[xb], b_identb], writes=[bb], inc=(k == 7))
            for k in range(8):
                bk, bb = pbs[k // 2]
                pv = bk[:].bitcast(BF16)
                c0 = (k % 2) * 512
                if (k // 2) % 2 == 0:
                    P.emit(act, lambda e, pv=pv, c0=c0, k=k: e.activation(
                        out=h2T[:, k, :], in_=pv[:, c0:c0 + 512], func=AF.Identity,
                        bias=ccols[:, CS2 + k:CS2 + k + 1], scale=ccols[:, CA2 + k:CA2 + k + 1]),
                        reads=[bb, b_ccols], writes=[b_h2T[k]])
                else:
                    P.emit(dve, lambda e, pv=pv, c0=c0, k=k: e.tensor_scalar(
                        out=h2T[:, k, :], in0=pv[:, c0:c0 + 512], scalar1=ccols[:, CA2 + k:CA2 + k + 1],
                        scalar2=ccols[:, CS2 + k:CS2 + k + 1], op0=ALU.mult, op1=ALU.add),
                        reads=[bb, b_ccols], writes=[b_h2T[k]])

        oc = [0]
        norm_transpose_b(0)
        for s in range(NST):
            for q in range(NLD):
                g = s * NLD + q
                wi_ensure(g + 1)
                if q == 2 and s + 1 < NST:
                    norm_transpose_b(s + 1, phase=1)
                sl = g % 2
                c0, w = load_cols(q)
                for jj in range(w // 128):
                    j = c0 // 128 + jj
                    bg, bbg = palloc()
                    bu, bbu = palloc()
                    for (bk, bb, g_) in ((bg, bbg, 0), (bu, bbu, 1)):
                        for k in range(8):
                            P.emit(pe, lambda e, bk=bk, k=k, g_=g_, sl=sl, jj=jj: e.matmul(
                                bk[:], lhsT=wi_ring[:, sl, k, g_, jj * 128:(jj + 1) * 128], rhs=h2T[:, k, :],
                                start=(k == 0), stop=(k == 7)), reads=[b_wi[sl], b_h2T[k]], writes=[bb], inc=(k == 7))
                    sb_ = j % 2
                    P.emit(act, lambda e, bg=bg, sb_=sb_: e.activation(out=sg[:, sb_, :], in_=bg[:], func=AF.Silu),
                           reads=[bbg], writes=[b_sg[sb_]])
                    P.emit(dve, lambda e, bu=bu, sb_=sb_, j=j: e.tensor_tensor(out=aT[:, j, :], in0=bu[:], in1=sg[:, sb_, :],
                                                                             op=ALU.mult),
                           reads=[bbu, b_sg[sb_]], writes=[b_aT[j]])
            if s + 1 < NST:
                norm_transpose_b(s + 1, phase=2)
            for i in range(4):
                ti = 4 * s + i
                pbk = []
                for hf in range(2):
                    bk, bb = palloc()
                    pbk.append((bk, bb))
                    for j in range(22):
                        P.emit(pe, lambda e, bk=bk, j=j, i=i, hf=hf: e.matmul(
                            bk[:], lhsT=aT[:, j, i * 128:(i + 1) * 128], rhs=wo_sb[:, j, hf * 512:(hf + 1) * 512],
                            start=(j == 0), stop=(j == 21)), reads=[b_aT[j], b_wo], writes=[bb], inc=(j == 21))
                ob = oc[0] % 2
                oc[0] += 1
                for hf in range(2):
                    bk, bb = pbk[hf]
                    sl_ = slice(hf * 512, (hf + 1) * 512)
                    P.emit(dve, lambda e, bk=bk, sl_=sl_, ob=ob: e.tensor_tensor(out=ost[:, ob, sl_], in0=bk[:],
                                                                                in1=gate2_bc[:, sl_], op=ALU.mult),
                           reads=[bb, b_condB], writes=[b_ost[ob]])
                P.emit(dve, lambda e, ti=ti, ob=ob: e.tensor_tensor(out=x_sb[:, ti, :], in0=x_sb[:, ti, :], in1=ost[:, ob, :],
                                                                   op=ALU.add),
                       reads=[b_x[ti], b_ost[ob]], writes=[b_x[ti]])
                P.emit(act, lambda e, ti=ti, i=i: e.activation(out=junk2[:], in_=x_sb[:, ti, :], func=AF.Square,
                                                               accum_out=stat[:, 16 + i:17 + i]),
                       reads=[b_x[ti]], writes=[b_junk2, b_stat[7]])
                P.emit(act, lambda e, i=i: e.activation(out=stat[:, 24 + i:25 + i], in_=stat[:, 16 + i:17 + i], func=AF.Ln,
                                                        bias=EPS, scale=1.0 / D), reads=[b_stat[7]], writes=[b_stat[8]])
                P.emit(act, lambda e, i=i: e.activation(out=stat[:, 24 + i:25 + i], in_=stat[:, 24 + i:25 + i], func=AF.Exp,
                                                        scale=-0.5), reads=[b_stat[8]], writes=[b_stat[8]])
                P.emit(dve, lambda e, ti=ti, ob=ob, i=i: e.scalar_tensor_tensor(
                    out=ost[:, ob, :], in0=x_sb[:, ti, :], scalar=stat[:, 24 + i:25 + i], in1=af_bc[:],
                    op0=ALU.mult, op1=ALU.mult), reads=[b_x[ti], b_stat[8], b_condB], writes=[b_ost[ob]])
                P.emit(dve, lambda e, ob=ob: e.tensor_tensor(out=ost[:, ob, :], in0=ost[:, ob, :], in1=sf_bc[:], op=ALU.add),
                       reads=[b_ost[ob], b_condB], writes=[b_ost[ob]])
                P.dma(sp, y_d[ti * 128:(ti + 1) * 128, :], ost[:, ob, :], d_out[ob], reads=[b_ost[ob]])


    except StopBuild:
        pass
    P._wait(sp, [(d.h, d.count) for d in d_out])
    P.barrier()
    for cm in reversed(ctxs):
        cm.__exit__(None, None, None)
    P.close()
    return nc


_NC_CACHE = {}


def _host_layout(inputs):
    f = lambda a: np.ascontiguousarray(np.asarray(a, dtype=np.float32))
    x = f(inputs["x"]); c = f(inputs["c"])
    col = lambda v: f(v).reshape(-1, 128).T
    rowbc = lambda v: np.broadcast_to(f(v).reshape(1, -1), (128, f(v).size))
    ada_b = f(inputs["ada_b"])[0]
    rows = np.empty((128, NROW), np.float32)
    rows[:, R_LNG:R_LNG + 512] = rowbc(inputs["a_ln_g"][0])
    rows[:, R_LNB:R_LNB + 512] = rowbc(inputs["a_ln_b"][0])
    bs = f(inputs["a_spatial_b"])[0]
    rows[:, R_BS:R_BS + 512] = np.repeat(bs.T[:, :, None], 64, axis=2).reshape(128, 512)
    rows[:, R_GF:R_GF + D] = rowbc(inputs["norm_f_g"])
    rows[:, R_ABG1:R_ABG1 + D] = rowbc(ada_b[2 * D:3 * D])
    rows[:, R_ABG2:R_ABG2 + D] = rowbc(ada_b[5 * D:6 * D])
    af_b = f(inputs["ada_f_b"])
    rows[:, R_ABSF:R_ABSF + D] = rowbc(af_b[0:D])
    rows[:, R_ABSCF:R_ABSCF + D] = rowbc(af_b[D:2 * D])
    b_in = f(inputs["b_in"])[0]
    rows[:, R_BINA:R_BINA + D] = rowbc(b_in[0:D])
    wsp = np.ascontiguousarray(f(inputs["a_spatial_w"])[0].transpose(2, 0, 1))
    shared = {
        "rows": rows, "wsp": wsp,
        "ada_w": f(inputs["ada_w"])[0], "ada_f_w": f(inputs["ada_f_w"]),
        "w_in": f(inputs["w_in"])[0], "w_out": f(inputs["w_out"])[0],
        "w_ffn_in": f(inputs["w_ffn_in"])[0], "w_ffn_out": f(inputs["w_ffn_out"])[0],
    }
    base_cols = np.zeros((128, NCOL), np.float32)
    base_cols[:, C_G1:C_G1 + 8] = col(inputs["norm1_g"][0])
    base_cols[:, C_G2:C_G2 + 8] = col(inputs["norm2_g"][0])
    base_cols[:, C_BB:C_BB + 8] = col(b_in[D:2 * D])
    base_cols[:, C_CB:C_CB + 4] = col(inputs["b_conv_b"][0])
    base_cols[:, C_GNG:C_GNG + 4] = col(inputs["b_gn_g"][0])
    base_cols[:, C_GNB:C_GNB + 4] = col(inputs["b_gn_b"][0])
    base_cols[:, C_GA:C_GA + 4] = col(inputs["out_norm_a_g"][0])
    base_cols[:, C_GB:C_GB + 4] = col(inputs["out_norm_b_g"][0])
    base_cols[:, C_AB + 0:C_AB + 8] = col(ada_b[0:D])
    base_cols[:, C_AB + 8:C_AB + 16] = col(ada_b[D:2 * D])
    base_cols[:, C_AB + 16:C_AB + 24] = col(ada_b[3 * D:4 * D])
    base_cols[:, C_AB + 24:C_AB + 32] = col(ada_b[4 * D:5 * D])
    cw = f(inputs["b_conv_w"])[0]
    for cc in range(4):
        base_cols[:, C_CW + cc * 31:C_CW + (cc + 1) * 31] = cw[:, cc * 128:(cc + 1) * 128].T
    in_maps = []
    for core in range(NCORES):
        b, q = core // 4, core % 4
        cols = base_cols.copy()
        cols[:, C_CC:C_CC + 8] = col(c[b])
        cols[:, C_HM] = 0.0 if q == 0 else 1.0
        xs = x[b, q * TOK:(q + 1) * TOK, :]
        if q == 0:
            xh = np.zeros((HALO, D), np.float32)
        else:
            xh = x[b, q * TOK - HALO:q * TOK, :]
        m = {"x": np.ascontiguousarray(xs), "xh": np.ascontiguousarray(xh), "cols": cols}
        m.update(shared)
        in_maps.append(m)
    return in_maps


def kernel(**inputs):
    if "nc" not in _NC_CACHE:
        _NC_CACHE["nc"] = build_nc()
    nc = _NC_CACHE["nc"]
    in_maps = _host_layout(inputs)
    res = run_bass_kernel_spmd(nc, in_maps, core_ids=list(range(NCORES)))
    out = np.empty((2, 4 * TOK, D), np.float32)
    for core in range(NCORES):
        b, q = core // 4, core % 4
        out[b, q * TOK:(q + 1) * TOK, :] = np.asarray(res.results[core]["y"], dtype=np.float32)
    return out
```
